# Optimizing a Trainium2 kernel written in Bass

```python
import math
import jax, jax.numpy as jnp
from jax import lax
import numpy as np

D_MODEL = 1024
BATCH = 32
SEQ = 2048
DEPTH = 2

H_A = 4
DH_A = 64
W_A = H_A * 2 * DH_A
Q_BLOCK = 128
ROPE_THETA = 10000.0
H_B = 8
N_B = 64
W_B = H_B * N_B
R_W = 64
R_A = 64
R_G = 160
RWKV_LN_EPS = 64e-5
H_C = 4
DK_C = 128
DV_C = 128
W_C = H_C * DK_C
HGRN_CHUNK = 32
D_FF = 2816
N_MOD = 9
EPS = 1e-6

P_A = 3 * W_A
P_B = 3 * W_B + R_W + R_A + R_G
P_C = 4 * W_C
P_G = 3 * D_MODEL
P_TOTAL = P_A + P_B + P_C + P_G

kernel_name = 'hybrid_diffattn_rwkv7_hgrn2_block'


def rms_norm(x, w, eps=EPS):
    xf = x.astype(jnp.float32)
    y = xf * lax.rsqrt(jnp.mean(xf * xf, axis=-1, keepdims=True) + eps)
    return (y * w.astype(jnp.float32)).astype(x.dtype)


def modulate(n, shift, scale):
    return n * (1.0 + scale) + shift


def swiglu(u, w_gate, w_up, w_down):
    return (jax.nn.silu(u @ w_gate) * (u @ w_up)) @ w_down


def rope(t, positions):
    dh = t.shape[-1]
    inv = ROPE_THETA ** (-jnp.arange(0, dh, 2, dtype=jnp.float32) / dh)
    ang = positions.astype(jnp.float32)[..., None] * inv
    cos = jnp.cos(ang)[:, :, None, None, :]
    sin = jnp.sin(ang)[:, :, None, None, :]
    tf = t.astype(jnp.float32)
    t1, t2 = tf[..., : dh // 2], tf[..., dh // 2:]
    return jnp.concatenate([t1 * cos - t2 * sin, t2 * cos + t1 * sin], axis=-1).astype(t.dtype)


def diff_attention(q, k, v, positions, qk_norm_w, lambda_qk, subln_w, layer_idx):
    B, S = q.shape[0], q.shape[1]
    q = rope(rms_norm(q, qk_norm_w[0]), positions)
    k = rope(rms_norm(k, qk_norm_w[1]), positions)
    lam_init = 0.8 - 0.6 * math.exp(-0.3 * layer_idx)
    lq = lambda_qk.astype(jnp.float32)
    lam = jnp.exp(jnp.sum(lq[0] * lq[1])) - jnp.exp(jnp.sum(lq[2] * lq[3])) + lam_init
    scale = DH_A ** -0.5
    outs = []
    for blk in range(S // Q_BLOCK):
        q0 = blk * Q_BLOCK
        kv_len = q0 + Q_BLOCK
        qb = q[:, q0:kv_len]
        kb = k[:, :kv_len]
        vb = v[:, :kv_len]
        s = jnp.einsum('bqhmd,bkhmd->bhmqk', qb, kb, preferred_element_type=jnp.float32) * scale
        causal = jnp.arange(kv_len)[None, :] <= (q0 + jnp.arange(Q_BLOCK))[:, None]
        p = jax.nn.softmax(jnp.where(causal, s, -jnp.inf), axis=-1)
        p = p[:, :, 0] - lam * p[:, :, 1]
        outs.append(jnp.einsum('bhqk,bkhe->bqhe', p.astype(v.dtype), vb))
    o = jnp.concatenate(outs, axis=1)
    o = rms_norm(o, subln_w) * (1.0 - lam_init)
    return o.reshape(B, S, W_A)


def token_shift(p):
    return jnp.pad(p, ((0, 0), (1, 0), (0, 0)))[:, :-1]


def rwkv7_scan(r, decay, k, v, kk, a):
    B, _, H, N = r.shape

    def step(state, inp):
        r_t, w_t, k_t, v_t, kk_t, a_t = inp
        s_kk = jnp.einsum('bhvk,bhk->bhv', state, kk_t)
        state = (state * w_t[:, :, None, :]
                 - s_kk[..., None] * (kk_t * a_t)[:, :, None, :]
                 + v_t[..., None] * k_t[:, :, None, :])
        return state, jnp.einsum('bhvk,bhk->bhv', state, r_t)

    xs = (jnp.moveaxis(t, 1, 0) for t in (r, decay, k, v, kk, a))
    _, out = lax.scan(step, jnp.zeros((B, H, N, N), jnp.float32), tuple(xs))
    return jnp.moveaxis(out, 0, 1)


def rwkv7_time_mix(p, mu, w0, w2, a0, a2, g2, k_k, k_a, r_k, ln_w, ln_b):
    B, S, _ = p.shape
    f32 = jnp.float32
    p = p + (token_shift(p) - p) * mu
    o1, o2, o3 = W_B, 2 * W_B, 3 * W_B
    r, k, v = p[..., :o1], p[..., o1:o2], p[..., o2:o3]
    xw = p[..., o3:o3 + R_W]
    xa = p[..., o3 + R_W:o3 + R_W + R_A]
    xg = p[..., o3 + R_W + R_A:]
    w = -jax.nn.softplus(-(w0 + jnp.tanh(xw) @ w2)) - 0.5
    decay = jnp.exp(-jnp.exp(w.astype(f32)))
    a = jax.nn.sigmoid(a0 + xa @ a2)
    g = jax.nn.sigmoid(xg) @ g2
    heads = lambda t: t.astype(f32).reshape(B, S, H_B, N_B)
    kk = heads(k * k_k)
    kk = kk / jnp.maximum(jnp.sqrt(jnp.sum(kk * kk, axis=-1, keepdims=True)), 1e-12)
    k = k * (1.0 + (a - 1.0) * k_a)
    rh, kh, vh, ah = heads(r), heads(k), heads(v), heads(a)
    o = rwkv7_scan(rh, heads(decay), kh, vh, kk, ah)
    mean = jnp.mean(o, axis=-1, keepdims=True)
    var = jnp.mean(jnp.square(o - mean), axis=-1, keepdims=True)
    o = ((o - mean) * lax.rsqrt(var + RWKV_LN_EPS)).reshape(B, S, W_B)
    o = o * ln_w.astype(f32) + ln_b.astype(f32)
    bonus = jnp.sum(rh * kh * r_k.astype(f32), axis=-1, keepdims=True) * vh
    o = o + bonus.reshape(B, S, W_B)
    return (o * g.astype(f32)).astype(p.dtype)


def hgrn2_chunked(q, k, v, log_f):
    B, S, H, dk = q.shape
    dv = v.shape[-1]
    C = HGRN_CHUNK
    n = S // C
    to_chunks = lambda t: t.reshape(B, n, C, H, t.shape[-1]).transpose(1, 0, 3, 2, 4)
    causal = jnp.tril(jnp.ones((C, C), dtype=bool))

    def step(state, inp):
        qc, kc, vc, lfc = inp
        b = jnp.cumsum(lfc, axis=2)
        o_inter = jnp.einsum('bhtk,bhkv->bhtv', qc * jnp.exp(b), state)
        diff = jnp.where(causal[:, :, None], b[:, :, :, None, :] - b[:, :, None, :, :], -jnp.inf)
        scores = jnp.einsum('bhtk,bhsk,bhtsk->bhts', qc, kc, jnp.exp(diff))
        o_intra = jnp.einsum('bhts,bhsv->bhtv', scores, vc)
        b_end = b[:, :, -1:, :]
        state = (jnp.exp(b_end[:, :, 0, :])[..., None] * state
                 + jnp.einsum('bhsk,bhsv->bhkv', kc * jnp.exp(b_end - b), vc))
        return state, o_inter + o_intra

    xs = tuple(to_chunks(t) for t in (q, k, v, log_f))
    _, o = lax.scan(step, jnp.zeros((B, H, dk, dv), jnp.float32), xs)
    return o.transpose(1, 0, 3, 2, 4).reshape(B, S, H, dv)


def hgrn2_mix(p, lower_bound, norm_w):
    B, S, _ = p.shape
    f32 = jnp.float32
    q, fz, i, g = jnp.split(p, 4, axis=-1)
    q = jax.nn.silu(q.astype(f32)).reshape(B, S, H_C, DK_C)
    lb = lower_bound.astype(f32)
    log_f = jnp.logaddexp(jnp.log(lb), jnp.log1p(-lb) + jax.nn.log_sigmoid(fz.astype(f32)))
    log_f = log_f.reshape(B, S, H_C, DK_C)
    k = -jnp.expm1(log_f)
    o = hgrn2_chunked(q, k, i.astype(f32).reshape(B, S, H_C, DV_C), log_f)
    o = rms_norm(o, norm_w) * jax.nn.silu(g.astype(f32).reshape(B, S, H_C, DV_C))
    return o.reshape(B, S, W_C).astype(p.dtype)


def hybrid_mixer(u, positions, layer_idx, w_in, qk_norm_w, lambda_qk, subln_w, w_out_a,
                 rwkv_mu, rwkv_w0, rwkv_w2, rwkv_a0, rwkv_a2, rwkv_g2, rwkv_k_k, rwkv_k_a,
                 rwkv_r_k, rwkv_ln_w, rwkv_ln_b, w_out_b, lower_bound, hgrn_norm_w, w_out_c, w_out):
    B, S, _ = u.shape
    p = u @ w_in
    pa = p[..., :P_A]
    pb = p[..., P_A:P_A + P_B]
    pc = p[..., P_A + P_B:P_A + P_B + P_C]
    pg = p[..., P_A + P_B + P_C:]
    qa = pa[..., :W_A].reshape(B, S, H_A, 2, DH_A)
    ka = pa[..., W_A:2 * W_A].reshape(B, S, H_A, 2, DH_A)
    va = pa[..., 2 * W_A:].reshape(B, S, H_A, 2 * DH_A)
    y_a = diff_attention(qa, ka, va, positions, qk_norm_w, lambda_qk, subln_w, layer_idx) @ w_out_a
    y_b = rwkv7_time_mix(pb, rwkv_mu, rwkv_w0, rwkv_w2, rwkv_a0, rwkv_a2, rwkv_g2,
                         rwkv_k_k, rwkv_k_a, rwkv_r_k, rwkv_ln_w, rwkv_ln_b) @ w_out_b
    y_c = hgrn2_mix(pc, lower_bound, hgrn_norm_w) @ w_out_c
    g_a, g_b, g_c = jnp.split(jax.nn.sigmoid(pg), 3, axis=-1)
    return (g_a * y_a + g_b * y_b + g_c * y_c) @ w_out


def setup_inputs(seed: int = 0) -> dict:
    key = jax.random.key(seed)
    ks = iter(jax.random.split(key, 40))
    f32 = jnp.float32
    L, D = DEPTH, D_MODEL

    def nrm(shape, scale):
        return jax.random.normal(next(ks), shape, f32) * scale

    def gain(shape):
        return 1.0 + nrm(shape, 0.02)

    x = nrm((BATCH, SEQ, D), 1.0)
    c = nrm((BATCH, D), 1.0)
    offsets = jax.random.randint(next(ks), (BATCH, 1), 0, 4096, dtype=jnp.int32)
    positions = (offsets + jnp.arange(SEQ, dtype=jnp.int32)[None, :]).astype(jnp.int32)
    return {
        'x': x,
        'c': c,
        'positions': positions,
        'mod_w': nrm((L, D, N_MOD * D), 0.5 * D ** -0.5),
        'mod_b': nrm((L, N_MOD * D), 0.02),
        'norm_w': gain((L, 3, D)),
        'ffn_w_gate': nrm((L, 2, D, D_FF), D ** -0.5),
        'ffn_w_up': nrm((L, 2, D, D_FF), D ** -0.5),
        'ffn_w_down': nrm((L, 2, D_FF, D), D_FF ** -0.5),
        'w_in': nrm((L, D, P_TOTAL), D ** -0.5),
        'qk_norm_w': gain((L, 2, DH_A)),
        'lambda_qk': nrm((L, 4, DH_A), 0.1),
        'subln_w': gain((L, 2 * DH_A)),
        'w_out_a': nrm((L, W_A, D), W_A ** -0.5),
        'rwkv_mu': jax.random.uniform(next(ks), (L, P_B), f32, 0.0, 1.0),
        'rwkv_w0': jax.random.uniform(next(ks), (L, W_B), f32, -6.0, 0.0),
        'rwkv_w2': nrm((L, R_W, W_B), R_W ** -0.5),
        'rwkv_a0': nrm((L, W_B), 0.1),
        'rwkv_a2': nrm((L, R_A, W_B), R_A ** -0.5),
        'rwkv_g2': nrm((L, R_G, W_B), R_G ** -0.5),
        'rwkv_k_k': 0.85 + nrm((L, W_B), 0.05),
        'rwkv_k_a': 1.0 + nrm((L, W_B), 0.05),
        'rwkv_r_k': nrm((L, H_B, N_B), 0.1),
        'rwkv_ln_w': gain((L, W_B)),
        'rwkv_ln_b': nrm((L, W_B), 0.02),
        'w_out_b': nrm((L, W_B, D), W_B ** -0.5),
        'hgrn_lower_bounds': nrm((L, W_C), 0.5),
        'hgrn_norm_w': gain((L, DV_C)),
        'w_out_c': nrm((L, W_C, D), W_C ** -0.5),
        'w_out': nrm((L, D, D), D ** -0.5),
    }


def reference(x, c, positions, mod_w, mod_b, norm_w, ffn_w_gate, ffn_w_up, ffn_w_down,
              w_in, qk_norm_w, lambda_qk, subln_w, w_out_a,
              rwkv_mu, rwkv_w0, rwkv_w2, rwkv_a0, rwkv_a2, rwkv_g2, rwkv_k_k, rwkv_k_a,
              rwkv_r_k, rwkv_ln_w, rwkv_ln_b, w_out_b,
              hgrn_lower_bounds, hgrn_norm_w, w_out_c, w_out):
    B, S, D = x.shape
    lb = jnp.cumsum(jax.nn.softmax(hgrn_lower_bounds.astype(jnp.float32), axis=0), axis=0)
    lb = lb - lb[0]
    cond = jax.nn.silu(c)
    h = x
    for l in range(DEPTH):
        mod = (cond @ mod_w[l] + mod_b[l]).reshape(B, N_MOD, D)
        m = [mod[:, j, None, :] for j in range(N_MOD)]
        u = modulate(rms_norm(h, norm_w[l, 0]), m[0], m[1])
        h = h + 0.5 * m[2] * swiglu(u, ffn_w_gate[l, 0], ffn_w_up[l, 0], ffn_w_down[l, 0])
        u = modulate(rms_norm(h, norm_w[l, 1]), m[3], m[4])
        y = hybrid_mixer(u, positions, l, w_in[l], qk_norm_w[l], lambda_qk[l], subln_w[l], w_out_a[l],
                         rwkv_mu[l], rwkv_w0[l], rwkv_w2[l], rwkv_a0[l], rwkv_a2[l], rwkv_g2[l],
                         rwkv_k_k[l], rwkv_k_a[l], rwkv_r_k[l], rwkv_ln_w[l], rwkv_ln_b[l], w_out_b[l],
                         lb[l], hgrn_norm_w[l], w_out_c[l], w_out[l])
        h = h + m[5] * y
        u = modulate(rms_norm(h, norm_w[l, 2]), m[6], m[7])
        h = h + 0.5 * m[8] * swiglu(u, ffn_w_gate[l, 1], ffn_w_up[l, 1], ffn_w_down[l, 1])
    return h
```

```python
import contextlib
import math
import numpy as np
import concourse.bass as bass
import concourse.mybir as mybir
from concourse.bass_utils import run_bass_kernel_spmd

F32 = mybir.dt.float32
BF16 = mybir.dt.bfloat16
I32 = mybir.dt.int32
AF = mybir.ActivationFunctionType
ALU = mybir.AluOpType
AX = mybir.AxisListType

D = 1024
DFF = 2816
NFF = DFF // 128
PTOT = 8480
EPS = 1e-6
RWKV_LN_EPS = 64e-5
TT = 512
CB = 64
CH = 32
NCH_IN = 67


class Cfg:
    def __init__(self, NB=4, S=2048, L=2, NCORES=8, stop_after=None, branches=(0, 1, 2)):
        self.branches = tuple(branches)
        self.NB, self.S, self.L, self.NCORES = NB, S, L, NCORES
        self.NT = NB * S
        self.stop_after = stop_after


class T:
    def __init__(self, h, name):
        self.h, self.name = h, name
        self.w = None
        self.rs = {}
        self.sem = None
        self.excl = False

    def __getitem__(self, k):
        return self.h[k]


class KB:
    def __init__(self, nc, ndsem=40):
        self.nc = nc
        self.E = {'pe': nc.tensor, 'act': nc.scalar, 'dve': nc.vector, 'pool': nc.gpsimd, 'sp': nc.sync}
        self.gs = contextlib.ExitStack()
        self.esem = {e: self.gs.enter_context(nc.semaphore("s_" + e)) for e in ('pe', 'act', 'dve', 'pool')}
        self.ecnt = {e: 0 for e in self.esem}
        self.seen = {q: {} for q in self.E}
        nsw = 16
        self.dsem = [[self.gs.enter_context(nc.semaphore("d%d" % i)), 0] for i in range(ndsem + nsw)]
        self.dfree = {'hw': list(range(ndsem)), 'sw': list(range(ndsem, ndsem + nsw))}
        self.dused = []
        self.ps = []
        for i in range(8):
            h = self.gs.enter_context(nc.psum_tensor("ps%d" % i, [128, 512], F32))
            self.ps.append(T(h, "ps%d" % i))
            self.ps[-1].excl = True
        self.pstack = None
        self.ninst = 0

    def gtile(self, name, shape, dt):
        return T(self.gs.enter_context(self.nc.sbuf_tensor(name, list(shape), dt)), name)

    def tile(self, name, shape, dt):
        self.uid = getattr(self, 'uid', 0) + 1
        name = "%s_%d" % (name, self.uid)
        return T(self.pstack.enter_context(self.nc.sbuf_tensor(name, list(shape), dt)), name)

    def dram(self, name, shape, dt, kind="Internal"):
        return T(self.nc.dram_tensor(name, list(shape), dt, kind=kind).ap(), name)

    @contextlib.contextmanager
    def phase(self):
        self.pstack = contextlib.ExitStack()
        try:
            yield
        finally:
            self.barrier()
            self.pstack.close()
            self.pstack = None

    def _wait(self, q, tok):
        kind, key, val = tok
        if kind == 'e':
            if key == q:
                if q == 'pe' or self.ecnt[q] - val >= 2:
                    return
            sk = ('e', key)
            if self.seen[q].get(sk, 0) >= val:
                return
            self.seen[q][sk] = val
            self.E[q].wait_ge(self.esem[key], val)
        else:
            sk = ('d', key)
            if self.seen[q].get(sk, 0) >= val:
                return
            self.seen[q][sk] = val
            self.E[q].wait_ge(self.dsem[key][0], val)
        self.ninst += 1

    def _deps(self, q, reads, writes):
        toks = {}
        for b in reads:
            if b.w is not None:
                k = b.w[:2]
                toks[k] = max(toks.get(k, 0), b.w[2])
            if b.excl:
                for k, v in b.rs.items():
                    if k != ('e', q):
                        toks[k] = max(toks.get(k, 0), v)
        for b in writes:
            if b.w is not None:
                k = b.w[:2]
                toks[k] = max(toks.get(k, 0), b.w[2])
            for k, v in b.rs.items():
                toks[k] = max(toks.get(k, 0), v)
        for (kind, key), v in toks.items():
            self._wait(q, (kind, key, v))

    def op(self, q, fn, reads=(), writes=()):
        self._deps(q, reads, writes)
        ins = fn(self.E[q])
        self.ecnt[q] += 1
        ins.then_inc(self.esem[q], 1)
        self.ninst += 1
        tok = ('e', q, self.ecnt[q])
        for b in reads:
            b.rs[('e', q)] = self.ecnt[q]
        for b in writes:
            b.w = tok
            b.rs = {}
        return ins

    def dma(self, q, out, in_, reads, writes, owner, **kw):
        kind = 'sw' if q == 'pool' else 'hw'
        if owner.sem is None:
            owner.sem = {}
        if kind not in owner.sem:
            owner.sem[kind] = self.dfree[kind].pop()
            self.dused.append((owner, kind))
        si = owner.sem[kind]
        self._deps(q, reads, writes)
        if self.dsem[si][1] > 0:
            self._wait(q, ('d', si, self.dsem[si][1]))
        ins = self.E[q].dma_start(out=out, in_=in_, **kw)
        self.dsem[si][1] += 16
        ins.then_inc(self.dsem[si][0], 16)
        self.ninst += 1
        tok = ('d', si, self.dsem[si][1])
        for b in reads:
            b.rs[('d', si)] = self.dsem[si][1]
        for b in writes:
            b.w = tok
            b.rs = {}
        return ins

    def barrier(self):
        for q in self.E:
            for e in self.esem:
                if e != q and self.ecnt[e] > 0:
                    self._wait(q, ('e', e, self.ecnt[e]))
            for si, (h, c) in enumerate(self.dsem):
                if c > 0:
                    self._wait(q, ('d', si, c))
        for o, kind in self.dused:
            self.dfree[kind].append(o.sem.pop(kind))
        self.dused = []

    def finish(self):
        self.barrier()
        self.gs.close()


def _consts():
    cols = {}
    parts = []
    off = [0]

    def add(name, arr):
        arr = np.asarray(arr, np.float32)
        cols[name] = (off[0], arr.shape[1])
        off[0] += arr.shape[1]
        parts.append(arr)

    p = np.arange(128)
    add('ident', np.eye(128))
    add('ones', np.ones((128, 128)))
    add('bones64', (p[:, None] // 64 == p[None, :] // 64))
    rot = np.zeros((128, 128))
    for i in range(128):
        g, d = i // 64, i % 64
        if d < 32:
            rot[g * 64 + d + 32, i] = -1.0
        else:
            rot[g * 64 + d - 32, i] = 1.0
    add('rot', rot)
    inv = (10000.0 ** (-(np.arange(0, 64, 2, dtype=np.float32)) / np.float32(64))).astype(np.float32)
    add('invf', inv[(p % 32)][:, None])
    add('maskA', (p[:, None] <= p[None, :]))
    s64 = p % 64
    j64 = np.arange(64)
    add('m_sl', (j64[None, :] < s64[:, None]))
    add('m_lt', (s64[:, None] < j64[None, :]))
    add('m_le', (s64[:, None] <= j64[None, :]))
    add('m_le2', (s64[:, None] <= j64[None, :]))
    j32 = np.arange(32)
    add('m_h', (p[:, None] % 32 <= j32[None, :]))
    add('m_h4', np.tile((p[:, None] % 32 <= j32[None, :]), (1, 4)))
    c = np.arange(TT)
    add('r64', np.broadcast_to((c % CB != 0)[None, :], (128, TT)))
    add('r32', np.broadcast_to((c % CH != 0)[None, :], (128, TT)))
    return np.concatenate(parts, axis=1).astype(np.float32), cols


CONSTS, CC = _consts()

PV = {}
_o = 0
for _n, _w in [('mod_b', 72), ('norm_w', 24), ('qkw', 2), ('subln', 1), ('lamq', 256), ('mu', 12), ('mu_wa', 1),
               ('mu_g1', 1), ('mu_g2', 1), ('w0', 4), ('a0', 4), ('k_k', 4), ('k_a', 4), ('r_k', 4), ('ln_w', 4),
               ('ln_b', 4), ('lbraw', 8), ('hnw', 1)]:
    PV[_n] = (_o, _w)
    _o += _w
NPV = _o


def _fm(v):
    v = np.asarray(v, np.float32)
    return np.ascontiguousarray(v.reshape(-1, 128).T)


def make_pv(inp, l, L):
    t = np.zeros((128, NPV), np.float32)

    def put(name, arr):
        o, w = PV[name]
        assert arr.shape == (128, w), (name, arr.shape)
        t[:, o:o + w] = arr

    put('mod_b', _fm(inp['mod_b'][l]))
    put('norm_w', np.concatenate([_fm(inp['norm_w'][l, i]) for i in range(3)], axis=1))
    qk = inp['qk_norm_w'][l]
    put('qkw', np.stack([np.tile(qk[0], 2), np.tile(qk[1], 2)], axis=1))
    put('subln', inp['subln_w'][l][:, None])
    put('lamq', np.broadcast_to(inp['lambda_qk'][l].reshape(1, 256), (128, 256)))
    mu = inp['rwkv_mu'][l]
    put('mu', _fm(mu[:1536]))
    put('mu_wa', mu[1536:1664][:, None])
    put('mu_g1', mu[1664:1792][:, None])
    g2 = np.zeros(128, np.float32)
    g2[:32] = mu[1792:1824]
    put('mu_g2', g2[:, None])
    put('w0', _fm(inp['rwkv_w0'][l]))
    put('a0', _fm(inp['rwkv_a0'][l]))
    put('k_k', _fm(inp['rwkv_k_k'][l]))
    put('k_a', _fm(inp['rwkv_k_a'][l]))
    put('r_k', _fm(inp['rwkv_r_k'][l].reshape(-1)))
    put('ln_w', _fm(inp['rwkv_ln_w'][l]))
    put('ln_b', _fm(inp['rwkv_ln_b'][l]))
    put('lbraw', np.concatenate([_fm(inp['hgrn_lower_bounds'][ll]) if ll < L else np.zeros((128, 4), np.float32)
                                 for ll in range(2)], axis=1))
    put('hnw', inp['hgrn_norm_w'][l][:, None])
    return t


def rot(lst, i):
    return lst[i % len(lst)]


class Prog:
    def __init__(self, cfg):
        self.cfg = cfg
        nc = self.nc = bass.Bass("TRN2", target_bir_lowering=False)
        NB, S, L, NT = cfg.NB, cfg.S, cfg.L, cfg.NT
        di = lambda n, s, d=F32: T(nc.dram_tensor(n, list(s), d, kind="ExternalInput").ap(), n)
        self.x = di("x", [NT, D])
        self.cT = di("cT", [128, 8, NB])
        self.pos = di("pos", [NB, S], I32)
        self.consts = di("consts", list(CONSTS.shape))
        self.pv = di("pv", [L, 128, NPV])
        self.w = {}
        for n, s in [('mod_w', [L, D, 9 * D]), ('ffn_w_gate', [L, 2, D, DFF]), ('ffn_w_up', [L, 2, D, DFF]),
                     ('ffn_w_down', [L, 2, DFF, D]), ('w_in', [L, D, PTOT]), ('w_out_a', [L, 512, D]),
                     ('w_out_b', [L, 512, D]), ('w_out_c', [L, 512, D]), ('w_out', [L, D, D]),
                     ('rwkv_w2', [L, 64, 512]), ('rwkv_a2', [L, 64, 512]), ('rwkv_g2', [L, 160, 512])]:
            self.w[n] = di(n, s)
        self.out = T(nc.dram_tensor("out", [NT, D], F32, kind="ExternalOutput").ap(), "out")
        kb = self.kb = KB(nc)
        self.hT = kb.dram("hT", [128, 8, NT], F32)
        self.pTr = [kb.dram("pT%d" % ch, [128, NT], F32) for ch in range(NCH_IN)]
        self.vtok = kb.dram("vtok", [NT, 512], BF16)
        self.oT = [kb.dram("oT%d" % i, [4, 128, NT], BF16) for i in range(3)]
        self.Wg = [[kb.dram("Wg%d_%d" % (l, i), [NFF, 128, 8 * 128], BF16) for i in range(2)] for l in range(L)]
        self.Wu = [[kb.dram("Wu%d_%d" % (l, i), [NFF, 128, 8 * 128], BF16) for i in range(2)] for l in range(L)]
        self.Wd = [[kb.dram("Wd%d_%d" % (l, i), [8, 128, NFF * 128], BF16) for i in range(2)] for l in range(L)]
        self.Win = [kb.dram("Win%d" % l, [NCH_IN, 128, 8 * 128], BF16) for l in range(L)]
        self.Wo = [[kb.dram("Wo%d_%d" % (l, i), [8, 128, 4 * 128], BF16) for i in range(3)] for l in range(L)]
        self.Woo = [kb.dram("Woo%d" % l, [8, 128, 8 * 128], BF16) for l in range(L)]
        self.cst = kb.gtile("cst", list(CONSTS.shape), F32)
        kb.dma('sp', self.cst[:], self.consts[:], [self.consts], [self.cst], self.cst)
        self.ones_bf = kb.gtile("ones_bf", [128, 128], BF16)
        kb.op('dve', lambda e: e.tensor_copy(self.ones_bf[:], self.C('ones')), [self.cst], [self.ones_bf])
        self.bones_bf = kb.gtile("bones_bf", [128, 128], BF16)
        kb.op('dve', lambda e: e.tensor_copy(self.bones_bf[:], self.C('bones64')), [self.cst], [self.bones_bf])
        self.maskA_bf = kb.gtile("maskA_bf", [128, 128], BF16)
        kb.op('dve', lambda e: e.tensor_copy(self.maskA_bf[:], self.C('maskA')), [self.cst], [self.maskA_bf])
        self.pvt = kb.gtile("pvt", [128, NPV], F32)
        self.modT = kb.gtile("modT", [128, 72, NB], F32)
        self.tabA = kb.gtile("tabA", [128, 3, 8, NB], F32)
        self.tabG = kb.gtile("tabG", [128, 3, 8, NB], F32)
        self.condT = kb.gtile("condT", [128, 8, NB], F32)
        self.lam = kb.gtile("lam", [128, 4], F32)
        self.lb = kb.gtile("lb", [128, 12], F32)
        self.epst = kb.gtile("epst", [128, 4], F32)
        for i_, v_ in enumerate((EPS, RWKV_LN_EPS, 1e-24, 0.0)):
            kb.op('pool', lambda e: e.memset(self.epst[:, i_:i_ + 1], v_), [], [self.epst])
        self.drained = False

    def C(self, name, rows=slice(0, 128)):
        o, w = CC[name]
        return self.cst[rows, o:o + w]

    def P(self, name, c=None, rows=slice(0, 128)):
        o, w = PV[name]
        if c is None:
            return self.pvt[rows, o:o + w]
        return self.pvt[rows, o + c:o + c + 1]

    def conv_weight(self, src, K, c0, ncols, dst, f0):
        with self.kb.phase():
            self._conv_weight(src, K, c0, ncols, dst, f0)

    def _conv_weight(self, src, K, c0, ncols, dst, f0):
        kb = self.kb
        KC = K // 128
        nf_tot = (ncols + 127) // 128
        nfb = max(1, 88 // KC)
        raws = [kb.tile("cw_raw%d" % i, [128, nfb * 128], F32) for i in range(3)]
        wsb = [kb.tile("cw_sb%d" % i, [128, nfb, KC, 128], BF16) for i in range(2)]
        engs = ['dve', 'act', 'pool']
        n = 0
        for bi, fb in enumerate(range(0, nf_tot, nfb)):
            nf = min(nfb, nf_tot - fb)
            cw = min(ncols - fb * 128, nf * 128)
            ws = rot(wsb, bi)
            if cw != nf * 128:
                kb.op('pool', lambda e: e.memset(ws[:], 0.0), [], [ws])
            for kc in range(KC):
                raw = rot(raws, n)
                kb.dma('sp', raw[:, 0:cw], src.h[kc * 128:(kc + 1) * 128, c0 + fb * 128:c0 + fb * 128 + cw],
                       [src], [raw], raw)
                q = engs[n % 3]
                n += 1
                if cw == nf * 128:
                    o_ap = ws[:, 0:nf, kc, :]
                    i_ap = raw[:, 0:cw].rearrange("p (f j) -> p f j", j=128)
                else:
                    assert nf == 1
                    o_ap = ws[:, 0, kc, 0:cw]
                    i_ap = raw[:, 0:cw]
                if q == 'act':
                    kb.op(q, lambda e: e.copy(o_ap, i_ap), [raw], [ws])
                else:
                    kb.op(q, lambda e: e.tensor_copy(o_ap, i_ap), [raw], [ws])
            kb.dma('pool', dst.h[f0 + fb:f0 + fb + nf].rearrange("f p x -> p f x"),
                   ws[:, 0:nf].rearrange("p f k j -> p f (k j)"), [ws], [dst], ws)

    def convert_weights(self, l):
        W = self.w
        sub = lambda n, *ix: T(W[n].h[ix], n)
        for i in range(2):
            self.conv_weight(sub('ffn_w_gate', l, i), D, 0, DFF, self.Wg[l][i], 0)
            self.conv_weight(sub('ffn_w_up', l, i), D, 0, DFF, self.Wu[l][i], 0)
            self.conv_weight(sub('ffn_w_down', l, i), DFF, 0, D, self.Wd[l][i], 0)
        win = sub('w_in', l)
        self.conv_weight(win, D, 0, 3328, self.Win[l], 0)
        self.conv_weight(win, D, 3360, 5120, self.Win[l], 27)
        self.conv_weight(win, D, 3328, 32, self.Win[l], 26)
        for i, n in enumerate(['w_out_a', 'w_out_b', 'w_out_c']):
            self.conv_weight(sub(n, l), 512, 0, D, self.Wo[l][i], 0)
        self.conv_weight(sub('w_out', l), D, 0, D, self.Woo[l], 0)

    def layer_setup(self, l):
        kb, cfg = self.kb, self.cfg
        NB = cfg.NB
        ps0 = kb.ps[0]
        with kb.phase():
            kb.dma('sp', self.pvt[:], self.pv.h[l], [self.pv], [self.pvt], self.pvt)
            if l == 0:
                craw = kb.tile("craw", [128, 8, NB], F32)
                kb.dma('sp', craw[:], self.cT[:], [self.cT], [craw], craw)
                kb.op('act', lambda e: e.activation(self.condT[:], craw[:], AF.Silu), [craw], [self.condT])
            wts = [kb.tile("modw%d" % i, [128, 8, 1024], F32) for i in range(2)]
            mw = self.w['mod_w'].h[l].rearrange("(c p) n -> p c n", p=128)
            for blk in range(9):
                wt = rot(wts, blk)
                kb.dma('sp', wt[:], mw[:, :, blk * 1024:(blk + 1) * 1024], [self.w['mod_w']], [wt], wt)
                for f in range(8):
                    ff = blk * 8 + f
                    for c in range(8):
                        kb.op('pe', lambda e: e.matmul(ps0[:, ff * NB:(ff + 1) * NB], wt[:, c, f * 128:(f + 1) * 128],
                                                       self.condT[:, c, :], start=(c == 0), stop=(c == 7)),
                              [wt, self.condT], [ps0])
            for f in range(72):
                kb.op('dve', lambda e: e.tensor_scalar(self.modT[:, f, :], ps0[:, f * NB:(f + 1) * NB],
                                                       self.P('mod_b', f), None, ALU.add), [ps0, self.pvt], [self.modT])
            for i in range(3):
                for c in range(8):
                    kb.op('dve', lambda e: e.tensor_scalar(self.tabA[:, i, c, :], self.modT[:, (3 * i + 1) * 8 + c, :],
                                                           1.0, self.P('norm_w', i * 8 + c), ALU.add, ALU.mult),
                          [self.modT, self.pvt], [self.tabA])
                    kb.op('dve', lambda e: e.tensor_scalar(self.tabG[:, i, c, :], self.modT[:, (3 * i + 2) * 8 + c, :],
                                                           (1.0 if i == 1 else 0.5), None, ALU.mult),
                          [self.modT], [self.tabG])
            lt = kb.tile("lamtmp", [128, 128], F32)
            ls = kb.tile("lamsum", [128, 4], F32)
            o = PV['lamq'][0]
            for j in range(2):
                kb.op('dve', lambda e: e.tensor_tensor(lt[:, j * 64:(j + 1) * 64], self.pvt[:, o + j * 128:o + j * 128 + 64],
                                                       self.pvt[:, o + j * 128 + 64:o + j * 128 + 128], ALU.mult),
                      [self.pvt], [lt])
                kb.op('dve', lambda e: e.reduce_sum(ls[:, j:j + 1], lt[:, j * 64:(j + 1) * 64], AX.X), [lt], [ls])
            kb.op('act', lambda e: e.activation(ls[:, 2:4], ls[:, 0:2], AF.Exp), [ls], [ls])
            lam_init = 0.8 - 0.6 * math.exp(-0.3 * l)
            kb.op('dve', lambda e: e.tensor_tensor(self.lam[:, 0:1], ls[:, 2:3], ls[:, 3:4], ALU.subtract), [ls], [self.lam])
            kb.op('dve', lambda e: e.tensor_scalar(self.lam[:, 0:1], self.lam[:, 0:1], lam_init, None, ALU.add),
                  [self.lam], [self.lam])
            kb.op('dve', lambda e: e.tensor_scalar(self.lam[:, 1:2], self.lam[:, 0:1], -1.0, None, ALU.mult),
                  [self.lam], [self.lam])
            le = kb.tile("lbe", [128, 12], F32)
            o = PV['lbraw'][0]
            if l == 0 or cfg.L == 1:
                kb.op('dve', lambda e: e.memset(self.lb[:, 0:4], 0.0), [], [self.lb])
            else:
                kb.op('act', lambda e: e.activation(le[:, 0:8], self.pvt[:, o:o + 8], AF.Exp), [self.pvt], [le])
                kb.op('dve', lambda e: e.tensor_tensor(le[:, 8:12], le[:, 0:4], le[:, 4:8], ALU.add), [le], [le])
                kb.op('dve', lambda e: e.reciprocal(le[:, 8:12], le[:, 8:12]), [le], [le])
                kb.op('dve', lambda e: e.tensor_tensor(self.lb[:, 0:4], le[:, 4:8], le[:, 8:12], ALU.mult), [le], [self.lb])
            kb.op('dve', lambda e: e.tensor_scalar(self.lb[:, 4:8], self.lb[:, 0:4], -1.0, 1.0, ALU.mult, ALU.add),
                  [self.lb], [self.lb])

    def xpose_in(self):
        kb, cfg = self.kb, self.cfg
        self.hTr = [T(self.hT.h[:, :, t * TT:(t + 1) * TT], "hTr%d" % t) for t in range(cfg.NT // TT)]
        with kb.phase():
            xts = [kb.tile("xt%d" % i, [128, D], F32) for i in range(3)]
            hts = [kb.tile("hti%d" % i, [128, 8, TT], F32) for i in range(2)]
            n = 0
            for t in range(cfg.NT // TT):
                ht = rot(hts, t)
                for tb in range(TT // 128):
                    xt = rot(xts, n)
                    r0 = t * TT + tb * 128
                    kb.dma('sp', xt[:], self.x.h[r0:r0 + 128, :], [self.x], [xt], xt)
                    for half in range(2):
                        ps = kb.ps[n % 8]
                        n += 1
                        for cc in range(4):
                            c = half * 4 + cc
                            kb.op('pe', lambda e: e.matmul(ps[:, cc * 128:(cc + 1) * 128], xt[:, c * 128:(c + 1) * 128],
                                                           self.C('ident'), start=True, stop=True), [xt, self.cst], [ps])
                        q = 'dve' if half == 0 else 'act'
                        o_ap = ht[:, half * 4:half * 4 + 4, tb * 128:(tb + 1) * 128]
                        i_ap = ps[:, :].rearrange("p (c t) -> p c t", t=128)
                        if q == 'dve':
                            kb.op(q, lambda e: e.tensor_copy(o_ap, i_ap), [ps], [ht])
                        else:
                            kb.op(q, lambda e: e.copy(o_ap, i_ap), [ps], [ht])
                kb.dma('pool', self.hTr[t][:], ht[:], [ht], [self.hTr[t]], ht)

    def xpose_out(self):
        kb, cfg = self.kb, self.cfg
        with kb.phase():
            xts = [kb.tile("xo%d" % i, [128, D], F32) for i in range(3)]
            hts = [kb.tile("hto%d" % i, [128, 8, TT], F32) for i in range(2)]
            n = 0
            for t in range(cfg.NT // TT):
                ht = rot(hts, t)
                kb.dma('sp', ht[:], self.hTr[t][:], [self.hTr[t]], [ht], ht)
                for tb in range(TT // 128):
                    xt = rot(xts, n)
                    for half in range(2):
                        ps = kb.ps[n % 8]
                        n += 1
                        for cc in range(4):
                            c = half * 4 + cc
                            kb.op('pe', lambda e: e.matmul(ps[:, cc * 128:(cc + 1) * 128], ht[:, c, tb * 128:(tb + 1) * 128],
                                                           self.C('ident'), start=True, stop=True), [ht, self.cst], [ps])
                        q = 'dve' if half == 0 else 'act'
                        o_ap = xt[:, half * 512:(half + 1) * 512]
                        if q == 'dve':
                            kb.op(q, lambda e: e.tensor_copy(o_ap, ps[:, :]), [ps], [xt])
                        else:
                            kb.op(q, lambda e: e.copy(o_ap, ps[:, :]), [ps], [xt])
                    r0 = t * TT + tb * 128
                    kb.dma('pool', self.out.h[r0:r0 + 128, :], xt[:], [xt], [self.out], xt)

    def norm_tiles(self):
        kb = self.kb
        return dict(sq=kb.tile("nm_sq", [128, 8, TT], BF16), rstd=kb.tile("nm_rstd", [128, TT], F32),
                    tmp=[kb.tile("nm_tmp%d" % i, [128, TT], F32) for i in range(3)])

    def norm_mod(self, ht, u, si, b, nt):
        kb = self.kb
        sq, rstd, tmps = nt['sq'], nt['rstd'], nt['tmp']
        ps = kb.ps[7]
        kb.op('act', lambda e: e.activation(sq[:], ht[:], AF.Square), [ht], [sq])
        for c in range(8):
            kb.op('pe', lambda e: e.matmul(ps[:, :], self.ones_bf[:], sq[:, c, :], start=(c == 0), stop=(c == 7)),
                  [self.ones_bf, sq], [ps])
        self.rsqrt(rstd, ps[:, :], [ps], 1.0 / D, EPS)
        for c in range(8):
            tmp = rot(tmps, c)
            kb.op('dve', lambda e: e.scalar_tensor_tensor(tmp[:], ht[:, c, :], self.tabA[:, si, c, b:b + 1], rstd[:],
                                                          ALU.mult, ALU.mult), [ht, self.tabA, rstd], [tmp])
            kb.op('act', lambda e: e.activation(u[:, c, :], tmp[:], AF.Identity,
                                                bias=self.modT[:, 3 * si * 8 + c, b:b + 1], scale=1.0),
                  [tmp, self.modT], [u])

    def ffn(self, l, i):
        kb, cfg = self.kb, self.cfg
        si = 0 if i == 0 else 2
        Wg, Wu, Wd = self.Wg[l][i], self.Wu[l][i], self.Wd[l][i]
        NTI = cfg.NT // TT
        with kb.phase():
            hts = [kb.tile("ht%d" % k, [128, 8, TT], F32) for k in range(2)]
            nt = self.norm_tiles()
            us = [kb.tile("u%d" % k, [128, 8, TT], BF16) for k in range(2)]
            act = kb.tile("act", [128, NFF, TT], BF16)
            sg = [kb.tile("sg%d" % k, [128, TT], F32) for k in range(2)]
            wg = [kb.tile("wg%d" % k, [128, 8, 128], BF16) for k in range(4)]
            wu = [kb.tile("wu%d" % k, [128, 8, 128], BF16) for k in range(4)]
            wd = [kb.tile("wd%d" % k, [128, NFF, 128], BF16) for k in range(2)]
            kb.dma('sp', hts[0][:], self.hTr[0][:], [self.hTr[0]], [hts[0]], hts[0])
            self.norm_mod(hts[0], us[0], si, 0, nt)
            for t in range(NTI):
                b = (t * TT) // cfg.S
                ht, u = rot(hts, t), rot(us, t)
                if t + 1 < NTI:
                    hn = rot(hts, t + 1)
                    kb.dma('sp', hn[:], self.hTr[t + 1][:], [self.hTr[t + 1]], [hn], hn)
                for f in range(NFF):
                    g, uu = rot(wg, f), rot(wu, f)
                    kb.dma('sp', g[:].rearrange("p c j -> p (c j)"), Wg.h[f], [Wg], [g], g)
                    kb.dma('sp', uu[:].rearrange("p c j -> p (c j)"), Wu.h[f], [Wu], [uu], uu)
                    pa, pb = kb.ps[(f % 2) * 2], kb.ps[(f % 2) * 2 + 1]
                    for c in range(8):
                        kb.op('pe', lambda e: e.matmul(pa[:, :], g[:, c, :], u[:, c, :], start=(c == 0), stop=(c == 7)),
                              [g, u], [pa])
                    for c in range(8):
                        kb.op('pe', lambda e: e.matmul(pb[:, :], uu[:, c, :], u[:, c, :], start=(c == 0), stop=(c == 7)),
                              [uu, u], [pb])
                    s = rot(sg, f)
                    kb.op('act', lambda e: e.activation(s[:], pa[:, :], AF.Silu), [pa], [s])
                    kb.op('dve', lambda e: e.tensor_tensor(act[:, f, :], s[:], pb[:, :], ALU.mult), [s, pb], [act])
                if t + 1 < NTI:
                    self.norm_mod(rot(hts, t + 1), rot(us, t + 1), si, ((t + 1) * TT) // cfg.S, nt)
                for c in range(8):
                    wdt = rot(wd, c)
                    kb.dma('sp', wdt[:].rearrange("p f j -> p (f j)"), Wd.h[c], [Wd], [wdt], wdt)
                    pc = kb.ps[4 + c % 2]
                    for f in range(NFF):
                        kb.op('pe', lambda e: e.matmul(pc[:, :], wdt[:, f, :], act[:, f, :], start=(f == 0),
                                                       stop=(f == NFF - 1)), [wdt, act], [pc])
                    kb.op('dve', lambda e: e.scalar_tensor_tensor(ht[:, c, :], pc[:, :], self.tabG[:, si, c, b:b + 1],
                                                                  ht[:, c, :], ALU.mult, ALU.add),
                          [pc, self.tabG, ht], [ht])
                kb.dma('pool', self.hTr[t][:], ht[:], [ht], [self.hTr[t]], ht)


def build(cfg):
    P = Prog(cfg)
    P.xpose_in()
    for l in range(cfg.L):
        P.convert_weights(l)
        P.layer_setup(l)
        P.ffn(l, 0)
        if cfg.stop_after == 'ffn1':
            break
        P.mixer(l)
        if cfg.stop_after == 'mixer':
            break
        P.ffn(l, 1)
    P.xpose_out()
    P.kb.finish()
    return P


WNAMES = ['mod_w', 'ffn_w_gate', 'ffn_w_up', 'ffn_w_down', 'w_in', 'w_out_a', 'w_out_b', 'w_out_c', 'w_out',
          'rwkv_w2', 'rwkv_a2', 'rwkv_g2']


def run(cfg, inputs):
    inp = {k: np.asarray(v) for k, v in inputs.items()}
    NB, S, L = cfg.NB, cfg.S, cfg.L
    P = build(cfg)
    pv = np.stack([make_pv(inp, l, L) for l in range(L)], axis=0)
    shared = {n: np.ascontiguousarray(inp[n], dtype=np.float32) for n in WNAMES}
    shared['consts'] = CONSTS
    shared['pv'] = pv
    in_maps = []
    for k in range(cfg.NCORES):
        sl = slice(k * NB, (k + 1) * NB)
        m = dict(shared)
        m['x'] = np.ascontiguousarray(inp['x'][sl].reshape(NB * S, D), dtype=np.float32)
        c = np.asarray(inp['c'][sl], np.float32)
        m['cT'] = np.ascontiguousarray(c.reshape(NB, 8, 128).transpose(2, 1, 0))
        m['pos'] = np.ascontiguousarray(inp['positions'][sl], dtype=np.int32)
        in_maps.append(m)
    res = run_bass_kernel_spmd(P.nc, in_maps, core_ids=list(range(cfg.NCORES)))
    outs = [np.asarray(r['out']).reshape(NB, S, D) for r in res.results]
    return np.concatenate(outs, axis=0).astype(np.float32)


def kernel(**inputs):
    return run(Cfg(), inputs)


def _inproj(self, l):
    kb, cfg = self.kb, self.cfg
    Win = self.Win[l]
    with kb.phase():
        hts = [kb.tile("ht%d" % k, [128, 8, TT], F32) for k in range(2)]
        nt = self.norm_tiles()
        us = [kb.tile("u%d" % k, [128, 8, TT], BF16) for k in range(2)]
        wc = [kb.tile("wc%d" % k, [128, 8, 128], BF16) for k in range(4)]
        ot = [kb.tile("ot%d" % k, [128, TT], F32) for k in range(4)]
        vt = [kb.tile("vt%d" % k, [128, 512], BF16) for k in range(2)]
        wv = kb.tile("wv", [128, 4, 8, 128], BF16)
        kb.dma('sp', wv[:].rearrange("p f c j -> p f (c j)"), Win.h[8:12].rearrange("f p x -> p f x"), [Win], [wv], wv)
        n = 0
        NTI = cfg.NT // TT
        kb.dma('sp', hts[0][:], self.hTr[0][:], [self.hTr[0]], [hts[0]], hts[0])
        self.norm_mod(hts[0], us[0], 1, 0, nt)
        for t in range(NTI):
            b = (t * TT) // cfg.S
            ht, u = rot(hts, t), rot(us, t)
            if t + 1 < NTI:
                hn = rot(hts, t + 1)
                kb.dma('sp', hn[:], self.hTr[t + 1][:], [self.hTr[t + 1]], [hn], hn)
            for ch in range(NCH_IN):
                if 8 <= ch < 12:
                    continue
                w = rot(wc, n)
                o = rot(ot, n)
                ps = kb.ps[n % 4]
                n += 1
                M = 32 if ch == 26 else 128
                kb.dma('sp', w[:].rearrange("p c j -> p (c j)"), Win.h[ch], [Win], [w], w)
                for c in range(8):
                    kb.op('pe', lambda e: e.matmul(ps[0:M, :], w[:, c, 0:M], u[:, c, :], start=(c == 0), stop=(c == 7)),
                          [w, u], [ps])
                if 27 <= ch < 31 or 39 <= ch < 43:
                    kb.op('act', lambda e: e.activation(o[0:M, :], ps[0:M, :], AF.Silu), [ps], [o])
                elif ch >= 43:
                    kb.op('act', lambda e: e.activation(o[0:M, :], ps[0:M, :], AF.Sigmoid), [ps], [o])
                elif n % 2 == 0:
                    kb.op('act', lambda e: e.copy(o[0:M, :], ps[0:M, :]), [ps], [o])
                else:
                    kb.op('dve', lambda e: e.tensor_copy(o[0:M, :], ps[0:M, :]), [ps], [o])
                kb.dma('pool', self.pTr[ch][0:M, t * TT:(t + 1) * TT], o[0:M, :], [o], [self.pTr[ch]], o)
            if t + 1 < NTI:
                self.norm_mod(rot(hts, t + 1), rot(us, t + 1), 1, ((t + 1) * TT) // cfg.S, nt)
            for tb in range(TT // 128):
                psv = kb.ps[4 + tb % 2]
                v = rot(vt, tb)
                for vc in range(4):
                    for c in range(8):
                        kb.op('pe', lambda e: e.matmul(psv[:, vc * 128:(vc + 1) * 128], u[:, c, tb * 128:(tb + 1) * 128],
                                                       wv[:, vc, c, :], start=(c == 0), stop=(c == 7)), [u, wv], [psv])
                kb.op('dve', lambda e: e.tensor_copy(v[:], psv[:, :]), [psv], [v])
                r0 = t * TT + tb * 128
                kb.dma('pool', self.vtok.h[r0:r0 + 128, :], v[:], [v], [self.vtok], v)


Prog.inproj = _inproj


def _rsqrt(self, out, in_ap, in_bufs, scale, eps):
    kb = self.kb
    col = {EPS: 0, RWKV_LN_EPS: 1, 1e-24: 2}[eps]
    kb.op('act', lambda e: e.activation(out[:], in_ap, AF.Ln, bias=self.epst[:, col:col + 1], scale=scale),
          in_bufs + [self.epst], [out])
    kb.op('act', lambda e: e.activation(out[:], out[:], AF.Exp, scale=-0.5), [out], [out])


Prog.rsqrt = _rsqrt


def _attention(self, l):
    kb, cfg = self.kb, self.cfg
    S, NB = cfg.S, cfg.NB
    NG = S // TT
    lam_init = 0.8 - 0.6 * math.exp(-0.3 * l)
    PI = math.pi
    with kb.phase():
        posi = kb.tile("posi", [128, S], I32)
        ang = kb.tile("ang", [128, S], F32)
        rt = kb.tile("rrt", [128, S], F32)
        ri = kb.tile("rri", [128, S], I32)
        cs = kb.tile("cos", [128, S], F32)
        sn = kb.tile("sin", [128, S], F32)
        raw = [kb.tile("qkraw%d" % i, [128, S], F32) for i in range(2)]
        qk = [kb.tile("qkbf%d" % i, [128, S], BF16) for i in range(2)]
        V = kb.tile("V", [128, S // 128, 128], BF16)
        sqs = [kb.tile("asq%d" % i, [128, TT], BF16) for i in range(3)]
        rstds = [kb.tile("arstd%d" % i, [128, TT], F32) for i in range(3)]
        xns = [kb.tile("axn%d" % i, [128, TT], F32) for i in range(3)]
        t1s = [kb.tile("at1%d" % i, [128, TT], F32) for i in range(3)]
        t2s = [kb.tile("at2%d" % i, [128, TT], F32) for i in range(3)]
        sq, rstd = sqs[0], rstds[0]
        npre = 0
        pts = [kb.tile("pt%d" % i, [128, TT], BF16) for i in range(4)]
        rl = [kb.tile("rl%d" % i, [128, TT], F32) for i in range(2)]
        oo = [kb.tile("oo%d" % i, [128, TT], F32) for i in range(2)]
        dd = kb.tile("dd", [128, TT], F32)
        ob = [kb.tile("ob%d" % i, [128, TT], BF16) for i in range(2)]
        sw = kb.tile("sw", [128, 1], F32)
        kb.op('dve', lambda e: e.tensor_scalar(sw[:], self.P('subln'), 1.0 - lam_init, None, ALU.mult), [self.pvt], [sw])
        n = 0
        for b in range(NB):
            kb.dma('sp', posi[:], self.pos.h[b:b + 1, :].partition_broadcast(128), [self.pos], [posi], posi)
            kb.op('dve', lambda e: e.tensor_copy(ang[:], posi[:]), [posi], [ang])
            kb.op('dve', lambda e: e.tensor_scalar(ang[:], ang[:], self.C('invf'), None, ALU.mult), [ang, self.cst], [ang])
            for (dst, shift) in ((sn, 0.0), (cs, 0.5 * PI)):
                C1 = 6.28125
                C2 = 2 * PI - C1
                kb.op('dve', lambda e: e.tensor_scalar(dst[:], ang[:], shift, None, ALU.add), [ang], [dst])
                kb.op('dve', lambda e: e.tensor_scalar(rt[:], dst[:], 1.0 / (2 * PI), None, ALU.mult), [dst], [rt])
                kb.op('dve', lambda e: e.tensor_copy(ri[:], rt[:]), [rt], [ri])
                kb.op('dve', lambda e: e.tensor_copy(rt[:], ri[:]), [ri], [rt])
                kb.op('dve', lambda e: e.scalar_tensor_tensor(dst[:], rt[:], -C1, dst[:], ALU.mult, ALU.add), [rt, dst], [dst])
                kb.op('dve', lambda e: e.scalar_tensor_tensor(dst[:], rt[:], -C2, dst[:], ALU.mult, ALU.add), [rt, dst], [dst])
                kb.op('dve', lambda e: e.tensor_scalar(rt[:], dst[:], PI, 2 * PI, ALU.is_gt, ALU.mult), [dst], [rt])
                kb.op('dve', lambda e: e.tensor_tensor(dst[:], dst[:], rt[:], ALU.subtract), [dst, rt], [dst])
                kb.op('dve', lambda e: e.tensor_scalar(dst[:], dst[:], -PI, PI, ALU.max, ALU.min), [dst], [dst])
                kb.op('act', lambda e: e.activation(dst[:], dst[:], AF.Sin), [dst], [dst])
            for h in range(4):
                tok = slice(b * S, (b + 1) * S)
                for j in range(2):
                    kb.dma('sp', raw[j][:], self.pTr[4 * j + h][:, tok], [self.pTr[4 * j + h]], [raw[j]], raw[j])
                kb.dma('sp', V[:], self.vtok.h[tok, h * 128:(h + 1) * 128].rearrange("(j p) e -> p j e", p=128),
                       [self.vtok], [V], V)
                for j in range(2):
                    for g in range(NG):
                        cs_ = slice(g * TT, (g + 1) * TT)
                        p6, p7 = kb.ps[(2 * npre) % 8], kb.ps[(2 * npre + 1) % 8]
                        sq, rstd, xn, t1, t2 = (rot(x_, npre) for x_ in (sqs, rstds, xns, t1s, t2s))
                        npre += 1
                        kb.op('act', lambda e: e.activation(sq[:], raw[j][:, cs_], AF.Square), [raw[j]], [sq])
                        kb.op('pe', lambda e: e.matmul(p6[:, :], self.bones_bf[:], sq[:], start=True, stop=True),
                              [self.bones_bf, sq], [p6])
                        self.rsqrt(rstd, p6[:, :], [p6], 1.0 / 64, EPS)
                        kb.op('dve', lambda e: e.scalar_tensor_tensor(xn[:], raw[j][:, cs_], self.P('qkw', j), rstd[:],
                                                                      ALU.mult, ALU.mult), [raw[j], self.pvt, rstd], [xn])
                        kb.op('pe', lambda e: e.matmul(p7[:, :], self.C('rot'), xn[:], start=True, stop=True),
                              [self.cst, xn], [p7])
                        kb.op('dve', lambda e: e.tensor_tensor(t1[:], xn[:], cs[:, cs_], ALU.mult), [xn, cs], [t1])
                        kb.op('dve', lambda e: e.tensor_tensor(t2[:], p7[:, :], sn[:, cs_], ALU.mult), [p7, sn], [t2])
                        kb.op('dve', lambda e: e.tensor_tensor(qk[j][:, cs_], t1[:], t2[:], ALU.add), [t1, t2], [qk[j]])
                q, k = qk
                pO = [kb.ps[2], kb.ps[3]]
                pL = [kb.ps[4], kb.ps[5]]
                its = [(G, m, j) for G in range(NG) for m in range(2) for j in range(4 * G + 4)]

                def emit_s(idx):
                    G, m, j = its[idx]
                    rows = slice(m * 64, (m + 1) * 64)
                    c0 = max(0, j - 4 * G) * 128
                    pS = kb.ps[idx % 2]
                    kb.op('pe', lambda e: e.matmul(pS[:, c0:TT], k[rows, j * 128:(j + 1) * 128],
                                                   q[rows, G * TT + c0:(G + 1) * TT], start=True, stop=True), [k, q], [pS])

                emit_s(0)
                for idx, (G, m, j) in enumerate(its):
                    nj = 4 * G + 4
                    r = j - 4 * G
                    c0 = max(0, r) * 128
                    pS = kb.ps[idx % 2]
                    pt = rot(pts, idx)
                    kb.op('act', lambda e: e.activation(pt[:, c0:TT], pS[:, c0:TT], AF.Exp, scale=0.125), [pS], [pt])
                    if idx + 1 < len(its):
                        emit_s(idx + 1)
                    if r >= 0:
                        kb.op('pool', lambda e: e.tensor_tensor(pt[:, c0:c0 + 128], pt[:, c0:c0 + 128],
                                                                self.maskA_bf[:], ALU.mult), [pt, self.maskA_bf], [pt])
                    kb.op('pe', lambda e: e.matmul(pO[m][:, c0:TT], V[:, j, :], pt[:, c0:TT], start=(j == 0),
                                                   stop=(j == nj - 1)), [V, pt], [pO[m]])
                    kb.op('pe', lambda e: e.matmul(pL[m][:, c0:TT], self.ones_bf[:], pt[:, c0:TT], start=(j == 0),
                                                   stop=(j == nj - 1)), [self.ones_bf, pt], [pL[m]])
                    if j != nj - 1:
                        continue
                    kb.op('act', lambda e: e.activation(rl[m][:], pL[m][:, :], AF.Ln), [pL[m]], [rl[m]])
                    kb.op('act', lambda e: e.activation(rl[m][:], rl[m][:], AF.Exp, scale=-1.0), [rl[m]], [rl[m]])
                    kb.op('dve', lambda e: e.tensor_tensor(oo[m][:], pO[m][:, :], rl[m][:], ALU.mult), [pO[m], rl[m]], [oo[m]])
                    if m != 1:
                        continue
                    kb.op('dve', lambda e: e.scalar_tensor_tensor(dd[:], oo[1][:], self.lam[:, 1:2], oo[0][:], ALU.mult, ALU.add),
                          [oo[1], oo[0], self.lam], [dd])
                    p6 = kb.ps[6 + G % 2]
                    sq, rstd = rot(sqs, G), rot(rstds, G)
                    kb.op('act', lambda e: e.activation(sq[:], dd[:], AF.Square), [dd], [sq])
                    kb.op('pe', lambda e: e.matmul(p6[:, :], self.ones_bf[:], sq[:], start=True, stop=True),
                          [self.ones_bf, sq], [p6])
                    self.rsqrt(rstd, p6[:, :], [p6], 1.0 / 128, EPS)
                    o = rot(ob, G)
                    kb.op('dve', lambda e: e.scalar_tensor_tensor(o[:], dd[:], sw[:, 0:1], rstd[:], ALU.mult, ALU.mult),
                          [dd, sw, rstd], [o])
                    kb.dma('pool', self.oT[0].h[h, :, b * S + G * TT:b * S + (G + 1) * TT], o[:], [o], [self.oT[0]], o)


Prog.attention = _attention


def _merge(self, l, branches=(0, 1, 2)):
    kb, cfg = self.kb, self.cfg
    with kb.phase():
        hts = [kb.tile("ht%d" % k, [128, 8, TT], F32) for k in range(2)]
        wo = [kb.tile("wo%d" % i, [128, 8, 4, 128], BF16) for i in range(3)]
        woo = kb.tile("woo", [128, 8, 8, 128], BF16)
        for i in branches:
            kb.dma('sp', wo[i][:].rearrange("p f k j -> p f (k j)"), self.Wo[l][i].h.rearrange("f p x -> p f x"),
                   [self.Wo[l][i]], [wo[i]], wo[i])
        kb.dma('sp', woo[:].rearrange("p f k j -> p f (k j)"), self.Woo[l].h.rearrange("f p x -> p f x"),
               [self.Woo[l]], [woo], woo)
        ob = [[kb.tile("mo%d_%d" % (i, k), [128, 4, TT], BF16) for k in range(2)] for i in range(3)]
        gt = [kb.tile("mg%d" % k, [128, TT], F32) for k in range(6)]
        tm = [kb.tile("mt%d" % k, [128, TT], F32) for k in range(3)]
        z = kb.tile("mz", [128, 8, TT], BF16)
        n = 0
        for t in range(cfg.NT // TT):
            b = (t * TT) // cfg.S
            tok = slice(t * TT, (t + 1) * TT)
            ht = rot(hts, t)
            kb.dma('sp', ht[:], self.hTr[t][:], [self.hTr[t]], [ht], ht)
            o = {}
            for i in branches:
                o[i] = rot(ob[i], t)
                kb.dma('sp', o[i][:], self.oT[i].h[:, :, tok].rearrange("f p t -> p f t"), [self.oT[i]], [o[i]], o[i])
            for f in range(8):
                first = True
                for i in branches:
                    ps = kb.ps[(n % 2) * 3 + i]
                    g = rot(gt, n * 3 + i)
                    ch = 43 + 8 * i + f
                    kb.dma('sp', g[:], self.pTr[ch][:, tok], [self.pTr[ch]], [g], g)
                    for kc in range(4):
                        kb.op('pe', lambda e: e.matmul(ps[:, :], wo[i][:, f, kc, :], o[i][:, kc, :], start=(kc == 0),
                                                       stop=(kc == 3)), [wo[i], o[i]], [ps])
                    if len(branches) == 1:
                        kb.op('dve', lambda e: e.tensor_tensor(z[:, f, :], ps[:, :], g[:], ALU.mult), [ps, g], [z])
                    elif first:
                        acc = rot(tm, n)
                        kb.op('dve', lambda e: e.tensor_tensor(acc[:], ps[:, :], g[:], ALU.mult), [ps, g], [acc])
                    else:
                        kb.op('dve', lambda e: e.tensor_tensor(g[:], ps[:, :], g[:], ALU.mult), [ps, g], [g])
                        last = (i == branches[-1])
                        if last:
                            kb.op('pool', lambda e: e.tensor_tensor(z[:, f, :], acc[:], g[:], ALU.add), [acc, g], [z])
                        else:
                            kb.op('pool', lambda e: e.tensor_tensor(acc[:], acc[:], g[:], ALU.add), [acc, g], [acc])
                    first = False
                n += 1
            for f in range(8):
                ps = kb.ps[6 + f % 2]
                for kc in range(8):
                    kb.op('pe', lambda e: e.matmul(ps[:, :], woo[:, f, kc, :], z[:, kc, :], start=(kc == 0), stop=(kc == 7)),
                          [woo, z], [ps])
                kb.op('dve', lambda e: e.scalar_tensor_tensor(ht[:, f, :], ps[:, :], self.tabG[:, 1, f, b:b + 1], ht[:, f, :],
                                                              ALU.mult, ALU.add), [ps, self.tabG, ht], [ht])
            kb.dma('pool', self.hTr[t][:], ht[:], [ht], [self.hTr[t]], ht)


Prog.merge = _merge


def _mixer(self, l):
    br = self.cfg.branches
    self.inproj(l)
    if 0 in br:
        self.attention(l)
    if 1 in br:
        self.rwkv(l)
    if 2 in br:
        self.hgrn(l)
    self.merge(l, br)


Prog.mixer = _mixer


def _hgrn(self, l):
    kb, cfg = self.kb, self.cfg
    S, NB = cfg.S, cfg.NB
    NTL = S // TT
    NCK = TT // CH
    with kb.phase():
        def mk(name, shape, dt, nbuf):
            return [[kb.tile("%s%d_%d" % (name, h, k), shape, dt) for k in range(nbuf)] for h in range(4)]
        qr, fz, iv, gg = (mk(nm, [128, TT], F32, 2) for nm in ("hq", "hf", "hi", "hg"))
        logf, kx, cc, qt, kt, kp = (mk(nm, [128, TT], F32, 1) for nm in ("hlf", "hkx", "hc", "hqt", "hkt", "hkp"))
        WE4 = kb.tile("hWE4", [128, 4, NCK], F32)
        St = [kb.tile("hS%d" % k, [128, 4, 128], F32) for k in range(2)]
        tmpS = kb.tile("htmpS", [128, 4, 128], F32)
        AM = [kb.tile("hAM%d" % k, [32, 128], F32) for k in range(2)]
        VT = [kb.tile("hVT%d" % k, [32, 512], F32) for k in range(2)]
        KT = [kb.tile("hKT%d" % k, [32, 512], F32) for k in range(2)]
        osb = kb.tile("hosb", [128, TT], F32)
        OSB = kb.tile("hOSB", [128, 4, TT], F32)
        sq = kb.tile("hsq", [128, TT], BF16)
        rstd = kb.tile("hrstd", [128, TT], F32)
        yb = [kb.tile("hyb%d" % k, [128, TT], BF16) for k in range(2)]
        RA = [kb.ps[0]] * 4
        RV = [kb.ps[1]] * 4
        RK = [kb.ps[2]] * 4
        RS = [kb.ps[3]] * 4
        pO = [kb.ps[4 + h] for h in range(4)]
        ident = self.C('ident')
        nn = 0
        for b in range(NB):
            par = 0
            kb.op('pool', lambda e: e.memset(St[0][:], 0.0), [], [St[0]])
            for tl in range(NTL):
                t = b * NTL + tl
                tok = slice(t * TT, (t + 1) * TT)
                k2 = t % 2
                for h in range(4):
                    for (dst, ch) in ((qr, 27), (fz, 31), (iv, 35), (gg, 39)):
                        d = dst[h][k2]
                        kb.dma('sp', d[:], self.pTr[ch + h][:, tok], [self.pTr[ch + h]], [d], d)
                for h in range(4):
                    lf, kxx, c, q_, k_, kp_ = logf[h][0], kx[h][0], cc[h][0], qt[h][0], kt[h][0], kp[h][0]
                    kb.op('act', lambda e: e.activation(lf[:], fz[h][k2][:], AF.Sigmoid), [fz[h][k2]], [lf])
                    kb.op('dve', lambda e: e.tensor_scalar(lf[:], lf[:], self.lb[:, 4 + h:5 + h], self.lb[:, h:h + 1],
                                                           ALU.mult, ALU.add), [lf, self.lb], [lf])
                    kb.op('dve', lambda e: e.tensor_scalar(kxx[:], lf[:], -1.0, 1.0, ALU.mult, ALU.add), [lf], [kxx])
                    kb.op('act', lambda e: e.activation(lf[:], lf[:], AF.Ln), [lf], [lf])
                    kb.op('dve', lambda e: e.tensor_tensor_scan(c[:], self.C('r32'), lf[:], 0.0, ALU.mult, ALU.add),
                          [self.cst, lf], [c])
                    kb.op('act', lambda e: e.activation(q_[:], c[:], AF.Exp), [c], [q_])
                    kb.op('dve', lambda e: e.tensor_tensor(q_[:], q_[:], qr[h][k2][:], ALU.mult), [q_, qr[h][k2]], [q_])
                    kb.op('act', lambda e: e.activation(k_[:], c[:], AF.Exp, scale=-1.0), [c], [k_])
                    kb.op('dve', lambda e: e.tensor_tensor(k_[:], k_[:], kxx[:], ALU.mult), [k_, kxx], [k_])
                    kb.op('act', lambda e: e.activation(WE4[:, h, :], c[:, CH - 1::CH], AF.Exp), [c], [WE4])
                    kb.op('dve', lambda e: e.tensor_tensor(kp_[:].rearrange("p (n j) -> p n j", j=CH),
                                                           k_[:].rearrange("p (n j) -> p n j", j=CH),
                                                           WE4[:, h, :].unsqueeze(2).to_broadcast([128, NCK, CH]), ALU.mult),
                          [k_, WE4], [kp_])
                def front(ci):
                    cs_ = slice(ci * CH, (ci + 1) * CH)
                    pa_ = ci % 2
                    bA, bV, bK = kb.ps[0 + pa_], kb.ps[2 + pa_], kb.ps[4 + pa_]
                    am_, vt_, kt_ = AM[pa_], VT[pa_], KT[pa_]
                    for h in range(4):
                        kb.op('pe', lambda e: e.matmul(bA[0:32, h * 32:(h + 1) * 32], kt[h][0][:, cs_], qt[h][0][:, cs_],
                                                       start=True, stop=True), [kt[h][0], qt[h][0]], [bA])
                    for h in range(4):
                        kb.op('pe', lambda e: e.matmul(bV[0:32, h * 128:(h + 1) * 128], iv[h][k2][:, cs_], ident,
                                                       start=True, stop=True), [iv[h][k2], self.cst], [bV])
                    for h in range(4):
                        kb.op('pe', lambda e: e.matmul(bK[0:32, h * 128:(h + 1) * 128], kp[h][0][:, cs_], ident,
                                                       start=True, stop=True), [kp[h][0], self.cst], [bK])
                    kb.op('dve', lambda e: e.tensor_tensor(am_[:], bA[0:32, 0:128], self.C('m_h4', slice(0, 32)), ALU.mult),
                          [bA, self.cst], [am_])
                    kb.op('act', lambda e: e.copy(vt_[:], bV[0:32, :]), [bV], [vt_])
                    kb.op('act', lambda e: e.copy(kt_[:], bK[0:32, :]), [bK], [kt_])

                front(0)
                for ci in range(NCK):
                    cs_ = slice(ci * CH, (ci + 1) * CH)
                    cur, nxt = St[par], St[1 - par]
                    pa_ = ci % 2
                    am_, vt_, kt_ = AM[pa_], VT[pa_], KT[pa_]
                    bS, bO = kb.ps[6], kb.ps[7]
                    if ci + 1 < NCK:
                        front(ci + 1)
                    for h in range(4):
                        hs = slice(h * 128, (h + 1) * 128)
                        oc = slice(h * CH, (h + 1) * CH)
                        kb.op('pe', lambda e: e.matmul(bO[:, oc], cur[:, h, :], qt[h][0][:, cs_], start=True, stop=False),
                              [cur, qt[h][0]], [bO])
                        kb.op('pe', lambda e: e.matmul(bO[:, oc], vt_[:, hs], am_[:, h * 32:(h + 1) * 32], start=False, stop=True),
                              [vt_, am_], [bO])
                        kb.op('pe', lambda e: e.matmul(bS[:, hs], kt_[:, hs], vt_[:, hs], start=True, stop=True),
                              [kt_, vt_], [bS])
                    kb.op('dve', lambda e: e.tensor_tensor(tmpS[:], cur[:], WE4[:, :, ci:ci + 1].to_broadcast([128, 4, 128]),
                                                           ALU.mult), [cur, WE4], [tmpS])
                    kb.op('dve', lambda e: e.tensor_tensor(nxt[:], tmpS[:], bS[:, :].rearrange("p (a v) -> p a v", v=128), ALU.add),
                          [tmpS, bS], [nxt])
                    kb.op('act', lambda e: e.copy(OSB[:, :, cs_], bO[:, 0:4 * CH].rearrange("p (a v) -> p a v", v=CH)), [bO], [OSB])
                    par ^= 1
                    nn += 1
                for h in range(4):
                    regs = [kb.ps[h]]
                    kb.op('act', lambda e: e.activation(sq[:], OSB[:, h, :], AF.Square), [OSB], [sq])
                    kb.op('pe', lambda e: e.matmul(kb.ps[h][:, :], self.ones_bf[:], sq[:], start=True, stop=True),
                          [self.ones_bf, sq], regs)
                    self.rsqrt(rstd, kb.ps[h][:, :], regs, 1.0 / 128, EPS)
                    kb.op('dve', lambda e: e.scalar_tensor_tensor(osb[:], OSB[:, h, :], self.P('hnw'), rstd[:], ALU.mult, ALU.mult),
                          [OSB, self.pvt, rstd], [osb])
                    y = rot(yb, h)
                    kb.op('pool', lambda e: e.tensor_tensor(y[:], osb[:], gg[h][k2][:], ALU.mult), [osb, gg[h][k2]], [y])
                    kb.dma('pool', self.oT[2].h[h, :, tok], y[:], [y], [self.oT[2]], y)


Prog.hgrn = _hgrn


def _rwkv(self, l):
    kb, cfg = self.kb, self.cfg
    S, NB = cfg.S, cfg.NB
    NTL = S // TT
    NCK = TT // CB
    W = self.w
    with kb.phase():
        F = lambda name, shape=(128, TT), dt=F32: kb.tile(name, list(shape), dt)
        w2sb, a2sb, g2a, g2b = F("w2sb", (128, 512)), F("a2sb", (128, 512)), F("g2a", (128, 512)), F("g2b", (32, 512))
        kb.dma('sp', w2sb[0:64, :], W['rwkv_w2'].h[l], [W['rwkv_w2']], [w2sb], w2sb)
        kb.dma('sp', a2sb[64:128, :], W['rwkv_a2'].h[l], [W['rwkv_a2']], [a2sb], a2sb)
        kb.dma('sp', g2a[:], W['rwkv_g2'].h[l, 0:128, :], [W['rwkv_g2']], [g2a], g2a)
        kb.dma('sp', g2b[:], W['rwkv_g2'].h[l, 128:160, :], [W['rwkv_g2']], [g2b], g2b)
        omka = F("omka", (128, 4))
        kb.op('dve', lambda e: e.tensor_scalar(omka[:], self.P('k_a'), -1.0, 1.0, ALU.mult, ALU.add), [self.pvt], [omka])
        raws = [[F("rraw%d_%d" % (j, k), (128, TT + 1)) for k in range(2)] for j in range(3)]
        lraw = [raws[j][0] for j in range(3)]
        xs = [F("lxs%d" % j) for j in range(3)]
        TH, SG1, SG2 = F("TH"), F("SG1"), F("SG2")
        LWt, AAt, G = F("LWt"), F("AAt"), F("G", (128, 4, TT))
        AL, BE, KA, RH, VV, BN = ([F("%s%d" % (nm, hp)) for hp in range(4)]
                                  for nm in ("AL", "BE", "KA", "RH", "VV", "BN"))
        WE = F("WE", (128, 4, NCK))
        t_r, t_k, tm1, tm2, tm3, tm4, tc = (F(nm) for nm in ("t_r", "t_k", "tm1", "tm2", "tm3", "tm4", "tc"))
        dtmp = F("dtmp")
        St = [F("St%d" % k, (128, 4, 64)) for k in range(2)]
        tmpS = F("tmpS", (128, 4, 64))
        BKV = [[F("BKV%d_%d" % (hp, k), (128, 192)) for k in range(2)] for hp in range(4)]
        NMM = [[F("NMM%d_%d" % (hp, k), (128, 192)) for k in range(2)] for hp in range(4)]
        TTt = [[F("Tt%d_%d" % (hp, k), (128, 128)) for k in range(7)] for hp in range(4)]
        P1 = [F("P1_%d" % hp, (128, 256)) for hp in range(4)]
        PP = [[F("PP%d_%d" % (hp, k), (128, 256)) for k in range(2)] for hp in range(4)]
        XS, US = F("XS", (128, 256)), F("US", (128, 256))
        OSB = F("OSB", (128, 4, TT))
        cen, sqv, rstd = tm1, tm2, tm4
        yb = [F("ryb%d" % k, (128, TT), BF16) for k in range(2)]
        for hp in range(4):
            kb.op('pool', lambda e: e.memset(P1[hp][:], 0.0), [], [P1[hp]])
        b0, b1, b2, b3, b4, b5, b6, b7 = kb.ps
        io = CC['ident'][0]
        mask3 = self.cst[:, CC['m_lt'][0]:CC['m_lt'][0] + 192]
        bones = self.C('bones64')
        NEGE = -math.exp(-0.5)
        tt_idx = [0] * 4
        import os
        DBG = int(os.environ.get('RWKV_DBG', '0'))

        def shift_lerp(dst_ap, dst, raw, mucol, rows=slice(0, 128)):
            kb.op('dve', lambda e: e.tensor_tensor(dtmp[rows, :], raw[rows, 0:TT], raw[rows, 1:TT + 1], ALU.subtract),
                  [raw], [dtmp])
            kb.op('dve', lambda e: e.scalar_tensor_tensor(dst_ap, dtmp[rows, :], mucol, raw[rows, 1:TT + 1], ALU.mult, ALU.add),
                  [dtmp, raw, self.pvt], [dst])

        def load_halo(raw, ch, t, tl, rows=slice(0, 128)):
            if tl == 0:
                kb.op('pool', lambda e: e.memset(raw[rows, 0:1], 0.0), [], [raw])
                kb.dma('sp', raw[rows, 1:TT + 1], self.pTr[ch][rows, t * TT:(t + 1) * TT], [self.pTr[ch]], [raw], raw)
            else:
                kb.dma('sp', raw[rows, :], self.pTr[ch][rows, t * TT - 1:(t + 1) * TT], [self.pTr[ch]], [raw], raw)

        def stage_a(ci, hps, k2, out):
            cs_ = slice(ci * CB, (ci + 1) * CB)
            for hp in hps:
                bk = kb.ps[hp]
                for hh in range(2):
                    rows = slice(hh * 64, hh * 64 + 64)
                    idr = self.cst[rows, io + hh * 64:io + hh * 64 + 64]
                    for jj, src in enumerate((BE, KA, VV)):
                        kb.op('pe', lambda e: e.matmul(bk[rows, jj * 64:(jj + 1) * 64], src[hp][rows, cs_], idr,
                                                       start=True, stop=True), [src[hp], self.cst], [bk])
                    for jj, (lh, rh) in enumerate(((AL, BE), (BE, AL), (KA, AL), (BE, RH), (KA, RH))):
                        kb.op('pe', lambda e: e.matmul(bk[rows, 192 + jj * 64:192 + (jj + 1) * 64], lh[hp][rows, cs_],
                                                       rh[hp][rows, cs_], start=True, stop=True), [lh[hp], rh[hp]], [bk])
            yield
            for hp in hps:
                bk = kb.ps[hp]
                kb.op('act', lambda e: e.copy(BKV[hp][k2][:], bk[:, 0:192]), [bk], [BKV[hp][k2]])
                for hh in range(2):
                    rows = slice(hh * 64, hh * 64 + 64)
                    kb.op('dve', lambda e: e.tensor_tensor(P1[hp][rows, hh * 64:hh * 64 + 64], bk[rows, 192:256],
                                                           self.C('m_sl', rows), ALU.mult), [bk, self.cst], [P1[hp]])
                    kb.op('dve', lambda e: e.tensor_tensor(P1[hp][rows, 128 + hh * 64:128 + hh * 64 + 64], bk[rows, 256:320],
                                                           self.C('m_lt', rows), ALU.mult), [bk, self.cst], [P1[hp]])
                kb.op('dve', lambda e: e.tensor_tensor(NMM[hp][k2][:], bk[:, 320:512], mask3, ALU.mult), [bk, self.cst],
                      [NMM[hp][k2]])
            yield
            cur = {}
            for hp in hps:
                tt = TTt[hp][tt_idx[hp] % 7]
                tt_idx[hp] += 1
                kb.op('pool', lambda e: e.tensor_tensor(tt[:], self.C('ident'), P1[hp][:, 128:256], ALU.add),
                      [self.cst, P1[hp]], [tt])
                cur[hp] = (P1[hp], tt)
            for rnd in range(6):
                for hp in hps:
                    bk = kb.ps[hp]
                    pp, tt = cur[hp]
                    P_, Pt_ = pp[:, 0:128], pp[:, 128:256]
                    if rnd >= 1:
                        kb.op('pe', lambda e: e.matmul(bk[:, 256:384], P_, tt[:], start=True, stop=True), [pp, tt], [bk])
                    if rnd <= 4:
                        kb.op('pe', lambda e: e.matmul(bk[:, 0:128], Pt_, P_, start=True, stop=True), [pp], [bk])
                    if rnd <= 3:
                        kb.op('pe', lambda e: e.matmul(bk[:, 128:256], P_, Pt_, start=True, stop=True), [pp], [bk])
                yield
                for hp in hps:
                    bk = kb.ps[hp]
                    pp, tt = cur[hp]
                    npp, ntt = pp, tt
                    if rnd <= 4:
                        npp = PP[hp][rnd % 2]
                        wdt = 256 if rnd <= 3 else 128
                        kb.op('act', lambda e: e.copy(npp[:, 0:wdt], bk[:, 0:wdt]), [bk], [npp])
                    if rnd >= 1:
                        ntt = TTt[hp][tt_idx[hp] % 7]
                        tt_idx[hp] += 1
                        kb.op('dve', lambda e: e.tensor_tensor(ntt[:], tt[:], bk[:, 256:384], ALU.add), [tt, bk], [ntt])
                    cur[hp] = (npp, ntt)
                yield
            for hp in hps:
                out[hp] = cur[hp][1]

        def stage_b(ci, k2, tts, par):
            cs_ = slice(ci * CB, (ci + 1) * CB)
            Sc, Sn = St[par], St[1 - par]
            for hp in range(4):
                hc = slice(hp * 64, hp * 64 + 64)
                for hh in range(2):
                    rows = slice(hh * 64, hh * 64 + 64)
                    kb.op('pe', lambda e: e.matmul(b4[rows, hc], AL[hp][rows, cs_], Sc[rows, hp, :], start=True, stop=False),
                          [AL[hp], Sc], [b4])
                    kb.op('pe', lambda e: e.matmul(b4[rows, hc], NMM[hp][k2][rows, 0:64], BKV[hp][k2][rows, 128:192],
                                                   start=False, stop=True), [NMM[hp][k2], BKV[hp][k2]], [b4])
            yield
            kb.op('act', lambda e: e.copy(XS[:], b4[:, 0:256]), [b4], [XS])
            yield
            for hp in range(4):
                hc = slice(hp * 64, hp * 64 + 64)
                kb.op('pe', lambda e: e.matmul(b5[:, hc], tts[hp][:], XS[:, hc], start=True, stop=True), [tts[hp], XS], [b5])
            yield
            kb.op('dve', lambda e: e.tensor_copy(US[:], b5[:, 0:256]), [b5], [US])
            yield
            for hp in range(4):
                hc = slice(hp * 64, hp * 64 + 64)
                for hh in range(2):
                    rows = slice(hh * 64, hh * 64 + 64)
                    kb.op('pe', lambda e: e.matmul(b6[rows, hc], BKV[hp][k2][rows, 0:64], US[rows, hc], start=True, stop=False),
                          [BKV[hp][k2], US], [b6])
                    kb.op('pe', lambda e: e.matmul(b6[rows, hc], BKV[hp][k2][rows, 64:128], BKV[hp][k2][rows, 128:192],
                                                   start=False, stop=True), [BKV[hp][k2]], [b6])
                    kb.op('pe', lambda e: e.matmul(b7[rows, hc], Sc[rows, hp, :], RH[hp][rows, cs_], start=True, stop=False),
                          [Sc, RH[hp]], [b7])
                    kb.op('pe', lambda e: e.matmul(b7[rows, hc], US[rows, hc], NMM[hp][k2][rows, 64:128], start=False, stop=False),
                          [US, NMM[hp][k2]], [b7])
                    kb.op('pe', lambda e: e.matmul(b7[rows, hc], BKV[hp][k2][rows, 128:192], NMM[hp][k2][rows, 128:192],
                                                   start=False, stop=True), [BKV[hp][k2], NMM[hp][k2]], [b7])
            yield
            kb.op('dve', lambda e: e.tensor_tensor(tmpS[:], Sc[:], b6[:, 0:256].rearrange("p (a v) -> p a v", v=64), ALU.add),
                  [Sc, b6], [tmpS])
            kb.op('dve', lambda e: e.tensor_tensor(Sn[:], tmpS[:], WE[:, :, ci:ci + 1].to_broadcast([128, 4, 64]), ALU.mult),
                  [tmpS, WE], [Sn])
            kb.op('act', lambda e: e.copy(OSB[:, :, cs_], b7[:, 0:256].rearrange("p (a v) -> p a v", v=64)), [b7], [OSB])

        for b in range(NB):
            par = 0
            kb.op('pool', lambda e: e.memset(St[0][:], 0.0), [], [St[0]])
            for tl in range(NTL):
                t = b * NTL + tl
                tok = slice(t * TT, (t + 1) * TT)
                load_halo(lraw[0], 24, t, tl)
                load_halo(lraw[1], 25, t, tl)
                load_halo(lraw[2], 26, t, tl, slice(0, 32))
                shift_lerp(xs[0][:], xs[0], lraw[0], self.P('mu_wa'))
                shift_lerp(xs[1][:], xs[1], lraw[1], self.P('mu_g1'))
                shift_lerp(xs[2][0:32, :], xs[2], lraw[2], self.P('mu_g2', None, slice(0, 32)), slice(0, 32))
                kb.op('act', lambda e: e.activation(TH[0:64, :], xs[0][0:64, :], AF.Tanh), [xs[0]], [TH])
                kb.op('act', lambda e: e.activation(SG1[:], xs[1][:], AF.Sigmoid), [xs[1]], [SG1])
                kb.op('act', lambda e: e.activation(SG2[0:32, :], xs[2][0:32, :], AF.Sigmoid), [xs[2]], [SG2])
                for hp in range(4):
                    k2_ = hp % 2
                    rr, rk_, rv = raws[0][k2_], raws[1][k2_], raws[2][k2_]
                    load_halo(rr, 12 + hp, t, tl)
                    load_halo(rk_, 16 + hp, t, tl)
                    load_halo(rv, 20 + hp, t, tl)
                    shift_lerp(t_r[:], t_r, rr, self.P('mu', hp))
                    shift_lerp(t_k[:], t_k, rk_, self.P('mu', 4 + hp))
                    shift_lerp(VV[hp][:], VV[hp], rv, self.P('mu', 8 + hp))
                    fc = hp
                    fcs = slice(fc * 128, (fc + 1) * 128)
                    kb.op('pe', lambda e: e.matmul(b4[:, :], w2sb[0:64, fcs], TH[0:64, :], start=True, stop=True), [w2sb, TH], [b4])
                    kb.op('act', lambda e: e.activation(LWt[:], b4[:, :], AF.Sigmoid, bias=self.P('w0', fc), scale=1.0),
                          [b4, self.pvt], [LWt])
                    kb.op('pe', lambda e: e.matmul(b5[:, :], a2sb[64:128, fcs], xs[0][64:128, :], start=True, stop=True),
                          [a2sb, xs[0]], [b5])
                    kb.op('act', lambda e: e.activation(AAt[:], b5[:, :], AF.Sigmoid, bias=self.P('a0', fc), scale=1.0),
                          [b5, self.pvt], [AAt])
                    kb.op('pe', lambda e: e.matmul(b6[:, :], g2a[:, fcs], SG1[:], start=True, stop=False), [g2a, SG1], [b6])
                    kb.op('pe', lambda e: e.matmul(b6[:, :], g2b[0:32, fcs], SG2[0:32, :], start=False, stop=True), [g2b, SG2], [b6])
                    kb.op('dve', lambda e: e.tensor_copy(G[:, fc, :], b6[:, :]), [b6], [G])
                    a_ = AAt[:]
                    kb.op('dve', lambda e: e.tensor_scalar(tm1[:], t_k[:], self.P('k_k', hp), None, ALU.mult), [t_k, self.pvt], [tm1])
                    kb.op('act', lambda e: e.activation(tm4[:], tm1[:], AF.Square), [tm1], [tm4])
                    kb.op('pe', lambda e: e.matmul(b4[:, :], bones, tm4[:], start=True, stop=True), [self.cst, tm4], [b4])
                    self.rsqrt(tm4, b4[:, :], [b4], 1.0, 1e-24)
                    kb.op('dve', lambda e: e.tensor_tensor(tm1[:], tm1[:], tm4[:], ALU.mult), [tm1, tm4], [tm1])
                    kb.op('dve', lambda e: e.tensor_scalar(tm2[:], a_, self.P('k_a', hp), omka[:, hp:hp + 1], ALU.mult, ALU.add),
                          [AAt, self.pvt, omka], [tm2])
                    kb.op('dve', lambda e: e.tensor_tensor(tm2[:], tm2[:], t_k[:], ALU.mult), [tm2, t_k], [tm2])
                    kb.op('dve', lambda e: e.tensor_tensor(tm3[:], tm1[:], a_, ALU.mult), [tm1, AAt], [tm3])
                    kb.op('dve', lambda e: e.scalar_tensor_tensor(tm4[:], t_r[:], self.P('r_k', hp), tm2[:], ALU.mult, ALU.mult),
                          [t_r, self.pvt, tm2], [tm4])
                    kb.op('pe', lambda e: e.matmul(b5[:, :], bones, tm4[:], start=True, stop=True), [self.cst, tm4], [b5])
                    kb.op('dve', lambda e: e.tensor_tensor(BN[hp][:], b5[:, :], VV[hp][:], ALU.mult), [b5, VV[hp]], [BN[hp]])
                    kb.op('dve', lambda e: e.tensor_tensor_scan(tc[:], self.C('r64'), LWt[:], 0.0, ALU.mult, ALU.add),
                          [self.cst, LWt], [tc])
                    kb.op('act', lambda e: e.activation(tm4[:], tc[:], AF.Exp, scale=NEGE), [tc], [tm4])
                    kb.op('dve', lambda e: e.tensor_tensor(RH[hp][:], t_r[:], tm4[:], ALU.mult), [t_r, tm4], [RH[hp]])
                    kb.op('act', lambda e: e.activation(WE[:, hp, :], tc[:, CB - 1::CB], AF.Exp, scale=NEGE), [tc], [WE])
                    kb.op('act', lambda e: e.activation(tm4[:], tc[:], AF.Exp, scale=-NEGE), [tc], [tm4])
                    kb.op('dve', lambda e: e.tensor_tensor(KA[hp][:], tm2[:], tm4[:], ALU.mult), [tm2, tm4], [KA[hp]])
                    kb.op('dve', lambda e: e.scalar_tensor_tensor(BE[hp][:], tm3[:], -1.0, tm4[:], ALU.mult, ALU.mult),
                          [tm3, tm4], [BE[hp]])
                    kb.op('dve', lambda e: e.tensor_tensor(tc[:], tc[:], LWt[:], ALU.subtract), [tc, LWt], [tc])
                    kb.op('act', lambda e: e.activation(tm4[:], tc[:], AF.Exp, scale=NEGE), [tc], [tm4])
                    kb.op('dve', lambda e: e.tensor_tensor(AL[hp][:], tm1[:], tm4[:], ALU.mult), [tm1, tm4], [AL[hp]])
                def drain(gens):
                    gens = list(gens)
                    while gens:
                        for g_ in list(gens):
                            try:
                                next(g_)
                            except StopIteration:
                                gens.remove(g_)

                tts = {}
                drain([stage_a(0, (0, 1, 2, 3), 0, tts)])
                for ci in range(NCK):
                    nxt = {}
                    gens = [stage_b(ci, ci % 2, tts, par)]
                    if ci + 1 < NCK:
                        gens.insert(0, stage_a(ci + 1, (0, 1, 2, 3), (ci + 1) % 2, nxt))
                    drain(gens)
                    tts = nxt
                    par ^= 1
                for hp in range(4):
                    o = OSB[:, hp, :]
                    kb.op('pe', lambda e: e.matmul(b4[:, :], bones, o, start=True, stop=True), [self.cst, OSB], [b4])
                    kb.op('dve', lambda e: e.scalar_tensor_tensor(cen[:], b4[:, :], -1.0 / 64, o, ALU.mult, ALU.add), [b4, OSB], [cen])
                    kb.op('act', lambda e: e.activation(sqv[:], cen[:], AF.Square), [cen], [sqv])
                    kb.op('pe', lambda e: e.matmul(b5[:, :], bones, sqv[:], start=True, stop=True), [self.cst, sqv], [b5])
                    self.rsqrt(rstd, b5[:, :], [b5], 1.0 / 64, RWKV_LN_EPS)
                    kb.op('dve', lambda e: e.tensor_tensor(cen[:], cen[:], rstd[:], ALU.mult), [cen, rstd], [cen])
                    kb.op('dve', lambda e: e.tensor_scalar(cen[:], cen[:], self.P('ln_w', hp), self.P('ln_b', hp), ALU.mult, ALU.add),
                          [cen, self.pvt], [cen])
                    kb.op('pool', lambda e: e.tensor_tensor(cen[:], cen[:], BN[hp][:], ALU.add), [cen, BN[hp]], [cen])
                    y = rot(yb, hp)
                    kb.op('dve', lambda e: e.tensor_tensor(y[:], cen[:], G[:, hp, :], ALU.mult), [cen, G], [y])
                    kb.dma('pool', self.oT[1].h[hp, :, tok], y[:], [y], [self.oT[1]], y)


Prog.rwkv = _rwkv
```

```python
import contextlib
import math
import numpy as np
import concourse.bass as bass
import concourse.mybir as mybir
from concourse.bass_utils import run_bass_kernel_spmd

F32 = mybir.dt.float32
BF16 = mybir.dt.bfloat16
I32 = mybir.dt.int32
AF = mybir.ActivationFunctionType
ALU = mybir.AluOpType
AX = mybir.AxisListType

D = 1024
DFF = 2816
NFF = DFF // 128
PTOT = 8480
EPS = 1e-6
RWKV_LN_EPS = 64e-5
TT = 512
CB = 64
CH = 32
NCH_IN = 67


class Cfg:
    def __init__(self, NB=4, S=2048, L=2, NCORES=8, stop_after=None, branches=(0, 1, 2)):
        self.branches = tuple(branches)
        self.NB, self.S, self.L, self.NCORES = NB, S, L, NCORES
        self.NT = NB * S
        self.stop_after = stop_after


class T:
    def __init__(self, h, name):
        self.h, self.name = h, name
        self.w = None
        self.rs = {}
        self.sem = None
        self.excl = False

    def __getitem__(self, k):
        return self.h[k]


class KB:
    def __init__(self, nc, ndsem=40):
        self.nc = nc
        self.E = {'pe': nc.tensor, 'act': nc.scalar, 'dve': nc.vector, 'pool': nc.gpsimd, 'sp': nc.sync}
        self.gs = contextlib.ExitStack()
        self.esem = {e: self.gs.enter_context(nc.semaphore("s_" + e)) for e in ('pe', 'act', 'dve', 'pool')}
        self.ecnt = {e: 0 for e in self.esem}
        self.seen = {q: {} for q in self.E}
        nsw = 16
        self.dsem = [[self.gs.enter_context(nc.semaphore("d%d" % i)), 0] for i in range(ndsem + nsw)]
        self.dfree = {'hw': list(range(ndsem)), 'sw': list(range(ndsem, ndsem + nsw))}
        self.dused = []
        self.ps = []
        for i in range(8):
            h = self.gs.enter_context(nc.psum_tensor("ps%d" % i, [128, 512], F32))
            self.ps.append(T(h, "ps%d" % i))
            self.ps[-1].excl = True
        self.pstack = None
        self.ninst = 0

    def gtile(self, name, shape, dt):
        return T(self.gs.enter_context(self.nc.sbuf_tensor(name, list(shape), dt)), name)

    def tile(self, name, shape, dt):
        self.uid = getattr(self, 'uid', 0) + 1
        name = "%s_%d" % (name, self.uid)
        return T(self.pstack.enter_context(self.nc.sbuf_tensor(name, list(shape), dt)), name)

    def dram(self, name, shape, dt, kind="Internal"):
        return T(self.nc.dram_tensor(name, list(shape), dt, kind=kind).ap(), name)

    @contextlib.contextmanager
    def phase(self):
        self.pstack = contextlib.ExitStack()
        try:
            yield
        finally:
            self.barrier()
            self.pstack.close()
            self.pstack = None

    def _wait(self, q, tok):
        kind, key, val = tok
        if kind == 'e':
            if key == q:
                if q == 'pe' or self.ecnt[q] - val >= 2:
                    return
            sk = ('e', key)
            if self.seen[q].get(sk, 0) >= val:
                return
            self.seen[q][sk] = val
            self.E[q].wait_ge(self.esem[key], val)
        else:
            sk = ('d', key)
            if self.seen[q].get(sk, 0) >= val:
                return
            self.seen[q][sk] = val
            self.E[q].wait_ge(self.dsem[key][0], val)
        self.ninst += 1

    def _deps(self, q, reads, writes):
        toks = {}
        for b in reads:
            if b.w is not None:
                k = b.w[:2]
                toks[k] = max(toks.get(k, 0), b.w[2])
            if b.excl:
                for k, v in b.rs.items():
                    if k != ('e', q):
                        toks[k] = max(toks.get(k, 0), v)
        for b in writes:
            if b.w is not None:
                k = b.w[:2]
                toks[k] = max(toks.get(k, 0), b.w[2])
            for k, v in b.rs.items():
                toks[k] = max(toks.get(k, 0), v)
        for (kind, key), v in toks.items():
            self._wait(q, (kind, key, v))

    def op(self, q, fn, reads=(), writes=()):
        self._deps(q, reads, writes)
        ins = fn(self.E[q])
        self.ecnt[q] += 1
        ins.then_inc(self.esem[q], 1)
        self.ninst += 1
        tok = ('e', q, self.ecnt[q])
        for b in reads:
            b.rs[('e', q)] = self.ecnt[q]
        for b in writes:
            b.w = tok
            b.rs = {}
        return ins

    def dma(self, q, out, in_, reads, writes, owner, **kw):
        kind = 'sw' if q == 'pool' else 'hw'
        if owner.sem is None:
            owner.sem = {}
        if kind not in owner.sem:
            owner.sem[kind] = self.dfree[kind].pop()
            self.dused.append((owner, kind))
        si = owner.sem[kind]
        self._deps(q, reads, writes)
        if self.dsem[si][1] > 0:
            self._wait(q, ('d', si, self.dsem[si][1]))
        ins = self.E[q].dma_start(out=out, in_=in_, **kw)
        self.dsem[si][1] += 16
        ins.then_inc(self.dsem[si][0], 16)
        self.ninst += 1
        tok = ('d', si, self.dsem[si][1])
        for b in reads:
            b.rs[('d', si)] = self.dsem[si][1]
        for b in writes:
            b.w = tok
            b.rs = {}
        return ins

    def barrier(self):
        for q in self.E:
            for e in self.esem:
                if e != q and self.ecnt[e] > 0:
                    self._wait(q, ('e', e, self.ecnt[e]))
            for si, (h, c) in enumerate(self.dsem):
                if c > 0:
                    self._wait(q, ('d', si, c))
        for o, kind in self.dused:
            self.dfree[kind].append(o.sem.pop(kind))
        self.dused = []

    def finish(self):
        self.barrier()
        self.gs.close()


def _consts():
    cols = {}
    parts = []
    off = [0]

    def add(name, arr):
        arr = np.asarray(arr, np.float32)
        cols[name] = (off[0], arr.shape[1])
        off[0] += arr.shape[1]
        parts.append(arr)

    p = np.arange(128)
    add('ident', np.eye(128))
    add('ones', np.ones((128, 128)))
    add('bones64', (p[:, None] // 64 == p[None, :] // 64))
    rot = np.zeros((128, 128))
    for i in range(128):
        g, d = i // 64, i % 64
        if d < 32:
            rot[g * 64 + d + 32, i] = -1.0
        else:
            rot[g * 64 + d - 32, i] = 1.0
    add('rot', rot)
    inv = (10000.0 ** (-(np.arange(0, 64, 2, dtype=np.float32)) / np.float32(64))).astype(np.float32)
    add('invf', inv[(p % 32)][:, None])
    add('maskA', (p[:, None] <= p[None, :]))
    s64 = p % 64
    j64 = np.arange(64)
    add('m_sl', (j64[None, :] < s64[:, None]))
    add('m_lt', (s64[:, None] < j64[None, :]))
    add('m_le', (s64[:, None] <= j64[None, :]))
    add('m_le2', (s64[:, None] <= j64[None, :]))
    j32 = np.arange(32)
    add('m_h', (p[:, None] % 32 <= j32[None, :]))
    add('m_h4', np.tile((p[:, None] % 32 <= j32[None, :]), (1, 4)))
    c = np.arange(TT)
    add('r64', np.broadcast_to((c % CB != 0)[None, :], (128, TT)))
    add('r32', np.broadcast_to((c % CH != 0)[None, :], (128, TT)))
    return np.concatenate(parts, axis=1).astype(np.float32), cols


CONSTS, CC = _consts()

PV = {}
_o = 0
for _n, _w in [('mod_b', 72), ('norm_w', 24), ('qkw', 2), ('subln', 1), ('lamq', 256), ('mu', 12), ('mu_wa', 1),
               ('mu_g1', 1), ('mu_g2', 1), ('w0', 4), ('a0', 4), ('k_k', 4), ('k_a', 4), ('r_k', 4), ('ln_w', 4),
               ('ln_b', 4), ('lbraw', 8), ('hnw', 1)]:
    PV[_n] = (_o, _w)
    _o += _w
NPV = _o


def _fm(v):
    v = np.asarray(v, np.float32)
    return np.ascontiguousarray(v.reshape(-1, 128).T)


def make_pv(inp, l, L):
    t = np.zeros((128, NPV), np.float32)

    def put(name, arr):
        o, w = PV[name]
        assert arr.shape == (128, w), (name, arr.shape)
        t[:, o:o + w] = arr

    put('mod_b', _fm(inp['mod_b'][l]))
    put('norm_w', np.concatenate([_fm(inp['norm_w'][l, i]) for i in range(3)], axis=1))
    qk = inp['qk_norm_w'][l]
    put('qkw', np.stack([np.tile(qk[0], 2), np.tile(qk[1], 2)], axis=1))
    put('subln', inp['subln_w'][l][:, None])
    put('lamq', np.broadcast_to(inp['lambda_qk'][l].reshape(1, 256), (128, 256)))
    mu = inp['rwkv_mu'][l]
    put('mu', _fm(mu[:1536]))
    put('mu_wa', mu[1536:1664][:, None])
    put('mu_g1', mu[1664:1792][:, None])
    g2 = np.zeros(128, np.float32)
    g2[:32] = mu[1792:1824]
    put('mu_g2', g2[:, None])
    put('w0', _fm(inp['rwkv_w0'][l]))
    put('a0', _fm(inp['rwkv_a0'][l]))
    put('k_k', _fm(inp['rwkv_k_k'][l]))
    put('k_a', _fm(inp['rwkv_k_a'][l]))
    put('r_k', _fm(inp['rwkv_r_k'][l].reshape(-1)))
    put('ln_w', _fm(inp['rwkv_ln_w'][l]))
    put('ln_b', _fm(inp['rwkv_ln_b'][l]))
    put('lbraw', np.concatenate([_fm(inp['hgrn_lower_bounds'][ll]) if ll < L else np.zeros((128, 4), np.float32)
                                 for ll in range(2)], axis=1))
    put('hnw', inp['hgrn_norm_w'][l][:, None])
    return t


def rot(lst, i):
    return lst[i % len(lst)]


class Prog:
    def __init__(self, cfg):
        self.cfg = cfg
        nc = self.nc = bass.Bass("TRN2", target_bir_lowering=False)
        NB, S, L, NT = cfg.NB, cfg.S, cfg.L, cfg.NT
        di = lambda n, s, d=F32: T(nc.dram_tensor(n, list(s), d, kind="ExternalInput").ap(), n)
        self.x = di("x", [NT, D])
        self.cT = di("cT", [128, 8, NB])
        self.pos = di("pos", [NB, S], I32)
        self.consts = di("consts", list(CONSTS.shape))
        self.pv = di("pv", [L, 128, NPV])
        self.w = {}
        for n, s in [('mod_w', [L, D, 9 * D]), ('ffn_w_gate', [L, 2, D, DFF]), ('ffn_w_up', [L, 2, D, DFF]),
                     ('ffn_w_down', [L, 2, DFF, D]), ('w_in', [L, D, PTOT]), ('w_out_a', [L, 512, D]),
                     ('w_out_b', [L, 512, D]), ('w_out_c', [L, 512, D]), ('w_out', [L, D, D]),
                     ('rwkv_w2', [L, 64, 512]), ('rwkv_a2', [L, 64, 512]), ('rwkv_g2', [L, 160, 512])]:
            self.w[n] = di(n, s)
        self.out = T(nc.dram_tensor("out", [NT, D], F32, kind="ExternalOutput").ap(), "out")
        kb = self.kb = KB(nc)
        self.hT = kb.dram("hT", [128, 8, NT], F32)
        self.pTr = [kb.dram("pT%d" % ch, [128, NT], F32) for ch in range(NCH_IN)]
        self.vtok = kb.dram("vtok", [NT, 512], BF16)
        self.oT = [kb.dram("oT%d" % i, [4, 128, NT], BF16) for i in range(3)]
        self.Wg = [[kb.dram("Wg%d_%d" % (l, i), [NFF, 128, 8 * 128], BF16) for i in range(2)] for l in range(L)]
        self.Wu = [[kb.dram("Wu%d_%d" % (l, i), [NFF, 128, 8 * 128], BF16) for i in range(2)] for l in range(L)]
        self.Wd = [[kb.dram("Wd%d_%d" % (l, i), [8, 128, NFF * 128], BF16) for i in range(2)] for l in range(L)]
        self.Win = [kb.dram("Win%d" % l, [NCH_IN, 128, 8 * 128], BF16) for l in range(L)]
        self.Wo = [[kb.dram("Wo%d_%d" % (l, i), [8, 128, 4 * 128], BF16) for i in range(3)] for l in range(L)]
        self.Woo = [kb.dram("Woo%d" % l, [8, 128, 8 * 128], BF16) for l in range(L)]
        self.cst = kb.gtile("cst", list(CONSTS.shape), F32)
        kb.dma('sp', self.cst[:], self.consts[:], [self.consts], [self.cst], self.cst)
        self.ones_bf = kb.gtile("ones_bf", [128, 128], BF16)
        kb.op('dve', lambda e: e.tensor_copy(self.ones_bf[:], self.C('ones')), [self.cst], [self.ones_bf])
        self.bones_bf = kb.gtile("bones_bf", [128, 128], BF16)
        kb.op('dve', lambda e: e.tensor_copy(self.bones_bf[:], self.C('bones64')), [self.cst], [self.bones_bf])
        self.maskA_bf = kb.gtile("maskA_bf", [128, 128], BF16)
        kb.op('dve', lambda e: e.tensor_copy(self.maskA_bf[:], self.C('maskA')), [self.cst], [self.maskA_bf])
        self.pvt = kb.gtile("pvt", [128, NPV], F32)
        self.modT = kb.gtile("modT", [128, 72, NB], F32)
        self.tabA = kb.gtile("tabA", [128, 3, 8, NB], F32)
        self.tabG = kb.gtile("tabG", [128, 3, 8, NB], F32)
        self.condT = kb.gtile("condT", [128, 8, NB], F32)
        self.lam = kb.gtile("lam", [128, 4], F32)
        self.lb = kb.gtile("lb", [128, 12], F32)
        self.epst = kb.gtile("epst", [128, 4], F32)
        for i_, v_ in enumerate((EPS, RWKV_LN_EPS, 1e-24, 0.0)):
            kb.op('pool', lambda e: e.memset(self.epst[:, i_:i_ + 1], v_), [], [self.epst])
        self.drained = False

    def C(self, name, rows=slice(0, 128)):
        o, w = CC[name]
        return self.cst[rows, o:o + w]

    def P(self, name, c=None, rows=slice(0, 128)):
        o, w = PV[name]
        if c is None:
            return self.pvt[rows, o:o + w]
        return self.pvt[rows, o + c:o + c + 1]

    def conv_weight(self, src, K, c0, ncols, dst, f0):
        with self.kb.phase():
            self._conv_weight(src, K, c0, ncols, dst, f0)

    def _conv_weight(self, src, K, c0, ncols, dst, f0):
        kb = self.kb
        KC = K // 128
        nf_tot = (ncols + 127) // 128
        nfb = max(1, 88 // KC)
        raws = [kb.tile("cw_raw%d" % i, [128, nfb * 128], F32) for i in range(3)]
        wsb = [kb.tile("cw_sb%d" % i, [128, nfb, KC, 128], BF16) for i in range(2)]
        engs = ['dve', 'act']
        n = 0
        for bi, fb in enumerate(range(0, nf_tot, nfb)):
            nf = min(nfb, nf_tot - fb)
            cw = min(ncols - fb * 128, nf * 128)
            ws = rot(wsb, bi)
            if cw != nf * 128:
                kb.op('pool', lambda e: e.memset(ws[:], 0.0), [], [ws])
            for kc in range(KC):
                raw = rot(raws, n)
                kb.dma('sp', raw[:, 0:cw], src.h[kc * 128:(kc + 1) * 128, c0 + fb * 128:c0 + fb * 128 + cw],
                       [src], [raw], raw)
                q = engs[n % 2]
                n += 1
                if cw == nf * 128:
                    o_ap = ws[:, 0:nf, kc, :]
                    i_ap = raw[:, 0:cw].rearrange("p (f j) -> p f j", j=128)
                else:
                    assert nf == 1
                    o_ap = ws[:, 0, kc, 0:cw]
                    i_ap = raw[:, 0:cw]
                if q == 'act':
                    kb.op(q, lambda e: e.copy(o_ap, i_ap), [raw], [ws])
                else:
                    kb.op(q, lambda e: e.tensor_copy(o_ap, i_ap), [raw], [ws])
            kb.dma('pool', dst.h[f0 + fb:f0 + fb + nf].rearrange("f p x -> p f x"),
                   ws[:, 0:nf].rearrange("p f k j -> p f (k j)"), [ws], [dst], ws)

    def convert_weights(self, l):
        W = self.w
        sub = lambda n, *ix: T(W[n].h[ix], n)
        for i in range(2):
            self.conv_weight(sub('ffn_w_gate', l, i), D, 0, DFF, self.Wg[l][i], 0)
            self.conv_weight(sub('ffn_w_up', l, i), D, 0, DFF, self.Wu[l][i], 0)
            self.conv_weight(sub('ffn_w_down', l, i), DFF, 0, D, self.Wd[l][i], 0)
        win = sub('w_in', l)
        self.conv_weight(win, D, 0, 3328, self.Win[l], 0)
        self.conv_weight(win, D, 3360, 5120, self.Win[l], 27)
        self.conv_weight(win, D, 3328, 32, self.Win[l], 26)
        for i, n in enumerate(['w_out_a', 'w_out_b', 'w_out_c']):
            self.conv_weight(sub(n, l), 512, 0, D, self.Wo[l][i], 0)
        self.conv_weight(sub('w_out', l), D, 0, D, self.Woo[l], 0)

    def layer_setup(self, l):
        kb, cfg = self.kb, self.cfg
        NB = cfg.NB
        ps0 = kb.ps[0]
        with kb.phase():
            kb.dma('sp', self.pvt[:], self.pv.h[l], [self.pv], [self.pvt], self.pvt)
            if l == 0:
                craw = kb.tile("craw", [128, 8, NB], F32)
                kb.dma('sp', craw[:], self.cT[:], [self.cT], [craw], craw)
                kb.op('act', lambda e: e.activation(self.condT[:], craw[:], AF.Silu), [craw], [self.condT])
            wts = [kb.tile("modw%d" % i, [128, 8, 1024], F32) for i in range(2)]
            mw = self.w['mod_w'].h[l].rearrange("(c p) n -> p c n", p=128)
            for blk in range(9):
                wt = rot(wts, blk)
                kb.dma('sp', wt[:], mw[:, :, blk * 1024:(blk + 1) * 1024], [self.w['mod_w']], [wt], wt)
                for f in range(8):
                    ff = blk * 8 + f
                    for c in range(8):
                        kb.op('pe', lambda e: e.matmul(ps0[:, ff * NB:(ff + 1) * NB], wt[:, c, f * 128:(f + 1) * 128],
                                                       self.condT[:, c, :], start=(c == 0), stop=(c == 7)),
                              [wt, self.condT], [ps0])
            for f in range(72):
                kb.op('dve', lambda e: e.tensor_scalar(self.modT[:, f, :], ps0[:, f * NB:(f + 1) * NB],
                                                       self.P('mod_b', f), None, ALU.add), [ps0, self.pvt], [self.modT])
            for i in range(3):
                for c in range(8):
                    kb.op('dve', lambda e: e.tensor_scalar(self.tabA[:, i, c, :], self.modT[:, (3 * i + 1) * 8 + c, :],
                                                           1.0, self.P('norm_w', i * 8 + c), ALU.add, ALU.mult),
                          [self.modT, self.pvt], [self.tabA])
                    kb.op('dve', lambda e: e.tensor_scalar(self.tabG[:, i, c, :], self.modT[:, (3 * i + 2) * 8 + c, :],
                                                           (1.0 if i == 1 else 0.5), None, ALU.mult),
                          [self.modT], [self.tabG])
            lt = kb.tile("lamtmp", [128, 128], F32)
            ls = kb.tile("lamsum", [128, 4], F32)
            o = PV['lamq'][0]
            for j in range(2):
                kb.op('dve', lambda e: e.tensor_tensor(lt[:, j * 64:(j + 1) * 64], self.pvt[:, o + j * 128:o + j * 128 + 64],
                                                       self.pvt[:, o + j * 128 + 64:o + j * 128 + 128], ALU.mult),
                      [self.pvt], [lt])
                kb.op('dve', lambda e: e.reduce_sum(ls[:, j:j + 1], lt[:, j * 64:(j + 1) * 64], AX.X), [lt], [ls])
            kb.op('act', lambda e: e.activation(ls[:, 2:4], ls[:, 0:2], AF.Exp), [ls], [ls])
            lam_init = 0.8 - 0.6 * math.exp(-0.3 * l)
            kb.op('dve', lambda e: e.tensor_tensor(self.lam[:, 0:1], ls[:, 2:3], ls[:, 3:4], ALU.subtract), [ls], [self.lam])
            kb.op('dve', lambda e: e.tensor_scalar(self.lam[:, 0:1], self.lam[:, 0:1], lam_init, None, ALU.add),
                  [self.lam], [self.lam])
            kb.op('dve', lambda e: e.tensor_scalar(self.lam[:, 1:2], self.lam[:, 0:1], -1.0, None, ALU.mult),
                  [self.lam], [self.lam])
            le = kb.tile("lbe", [128, 12], F32)
            o = PV['lbraw'][0]
            if l == 0 or cfg.L == 1:
                kb.op('dve', lambda e: e.memset(self.lb[:, 0:4], 0.0), [], [self.lb])
            else:
                kb.op('act', lambda e: e.activation(le[:, 0:8], self.pvt[:, o:o + 8], AF.Exp), [self.pvt], [le])
                kb.op('dve', lambda e: e.tensor_tensor(le[:, 8:12], le[:, 0:4], le[:, 4:8], ALU.add), [le], [le])
                kb.op('dve', lambda e: e.reciprocal(le[:, 8:12], le[:, 8:12]), [le], [le])
                kb.op('dve', lambda e: e.tensor_tensor(self.lb[:, 0:4], le[:, 4:8], le[:, 8:12], ALU.mult), [le], [self.lb])
            kb.op('dve', lambda e: e.tensor_scalar(self.lb[:, 4:8], self.lb[:, 0:4], -1.0, 1.0, ALU.mult, ALU.add),
                  [self.lb], [self.lb])

    def xpose_in(self):
        kb, cfg = self.kb, self.cfg
        self.hTr = [T(self.hT.h[:, :, t * TT:(t + 1) * TT], "hTr%d" % t) for t in range(cfg.NT // TT)]
        with kb.phase():
            xts = [kb.tile("xt%d" % i, [128, D], F32) for i in range(3)]
            hts = [kb.tile("hti%d" % i, [128, 8, TT], F32) for i in range(2)]
            n = 0
            for t in range(cfg.NT // TT):
                ht = rot(hts, t)
                for tb in range(TT // 128):
                    xt = rot(xts, n)
                    r0 = t * TT + tb * 128
                    kb.dma('sp', xt[:], self.x.h[r0:r0 + 128, :], [self.x], [xt], xt)
                    for half in range(2):
                        ps = kb.ps[n % 8]
                        n += 1
                        for cc in range(4):
                            c = half * 4 + cc
                            kb.op('pe', lambda e: e.matmul(ps[:, cc * 128:(cc + 1) * 128], xt[:, c * 128:(c + 1) * 128],
                                                           self.C('ident'), start=True, stop=True), [xt, self.cst], [ps])
                        q = 'dve' if half == 0 else 'act'
                        o_ap = ht[:, half * 4:half * 4 + 4, tb * 128:(tb + 1) * 128]
                        i_ap = ps[:, :].rearrange("p (c t) -> p c t", t=128)
                        if q == 'dve':
                            kb.op(q, lambda e: e.tensor_copy(o_ap, i_ap), [ps], [ht])
                        else:
                            kb.op(q, lambda e: e.copy(o_ap, i_ap), [ps], [ht])
                kb.dma('pool', self.hTr[t][:], ht[:], [ht], [self.hTr[t]], ht)

    def xpose_out(self):
        kb, cfg = self.kb, self.cfg
        with kb.phase():
            xts = [kb.tile("xo%d" % i, [128, D], F32) for i in range(3)]
            hts = [kb.tile("hto%d" % i, [128, 8, TT], F32) for i in range(2)]
            n = 0
            for t in range(cfg.NT // TT):
                ht = rot(hts, t)
                kb.dma('sp', ht[:], self.hTr[t][:], [self.hTr[t]], [ht], ht)
                for tb in range(TT // 128):
                    xt = rot(xts, n)
                    for half in range(2):
                        ps = kb.ps[n % 8]
                        n += 1
                        for cc in range(4):
                            c = half * 4 + cc
                            kb.op('pe', lambda e: e.matmul(ps[:, cc * 128:(cc + 1) * 128], ht[:, c, tb * 128:(tb + 1) * 128],
                                                           self.C('ident'), start=True, stop=True), [ht, self.cst], [ps])
                        q = 'dve' if half == 0 else 'act'
                        o_ap = xt[:, half * 512:(half + 1) * 512]
                        if q == 'dve':
                            kb.op(q, lambda e: e.tensor_copy(o_ap, ps[:, :]), [ps], [xt])
                        else:
                            kb.op(q, lambda e: e.copy(o_ap, ps[:, :]), [ps], [xt])
                    r0 = t * TT + tb * 128
                    kb.dma('pool', self.out.h[r0:r0 + 128, :], xt[:], [xt], [self.out], xt)

    def norm_tiles(self):
        kb = self.kb
        return dict(sq=kb.tile("nm_sq", [128, 8, TT], BF16), rstd=kb.tile("nm_rstd", [128, TT], F32),
                    tmp=[kb.tile("nm_tmp%d" % i, [128, TT], F32) for i in range(3)])

    def norm_mod(self, ht, u, si, b, nt):
        kb = self.kb
        sq, rstd, tmps = nt['sq'], nt['rstd'], nt['tmp']
        ps = kb.ps[7]
        kb.op('act', lambda e: e.activation(sq[:], ht[:], AF.Square), [ht], [sq])
        for c in range(8):
            kb.op('pe', lambda e: e.matmul(ps[:, :], self.ones_bf[:], sq[:, c, :], start=(c == 0), stop=(c == 7)),
                  [self.ones_bf, sq], [ps])
        self.rsqrt(rstd, ps[:, :], [ps], 1.0 / D, EPS)
        for c in range(8):
            tmp = rot(tmps, c)
            kb.op('dve', lambda e: e.scalar_tensor_tensor(tmp[:], ht[:, c, :], self.tabA[:, si, c, b:b + 1], rstd[:],
                                                          ALU.mult, ALU.mult), [ht, self.tabA, rstd], [tmp])
            kb.op('act', lambda e: e.activation(u[:, c, :], tmp[:], AF.Identity,
                                                bias=self.modT[:, 3 * si * 8 + c, b:b + 1], scale=1.0),
                  [tmp, self.modT], [u])

    def ffn(self, l, i):
        kb, cfg = self.kb, self.cfg
        si = 0 if i == 0 else 2
        Wg, Wu, Wd = self.Wg[l][i], self.Wu[l][i], self.Wd[l][i]
        NTI = cfg.NT // TT
        with kb.phase():
            hts = [kb.tile("ht%d" % k, [128, 8, TT], F32) for k in range(2)]
            nt = self.norm_tiles()
            us = [kb.tile("u%d" % k, [128, 8, TT], BF16) for k in range(2)]
            act = kb.tile("act", [128, NFF, TT], BF16)
            sg = [kb.tile("sg%d" % k, [128, TT], F32) for k in range(2)]
            wg = [kb.tile("wg%d" % k, [128, 8, 128], BF16) for k in range(4)]
            wu = [kb.tile("wu%d" % k, [128, 8, 128], BF16) for k in range(4)]
            wd = [kb.tile("wd%d" % k, [128, NFF, 128], BF16) for k in range(2)]
            kb.dma('sp', hts[0][:], self.hTr[0][:], [self.hTr[0]], [hts[0]], hts[0])
            self.norm_mod(hts[0], us[0], si, 0, nt)
            for t in range(NTI):
                b = (t * TT) // cfg.S
                ht, u = rot(hts, t), rot(us, t)
                if t + 1 < NTI:
                    hn = rot(hts, t + 1)
                    kb.dma('sp', hn[:], self.hTr[t + 1][:], [self.hTr[t + 1]], [hn], hn)
                for f in range(NFF):
                    g, uu = rot(wg, f), rot(wu, f)
                    kb.dma('sp', g[:].rearrange("p c j -> p (c j)"), Wg.h[f], [Wg], [g], g)
                    kb.dma('sp', uu[:].rearrange("p c j -> p (c j)"), Wu.h[f], [Wu], [uu], uu)
                    pa, pb = kb.ps[(f % 2) * 2], kb.ps[(f % 2) * 2 + 1]
                    for c in range(8):
                        kb.op('pe', lambda e: e.matmul(pa[:, :], g[:, c, :], u[:, c, :], start=(c == 0), stop=(c == 7)),
                              [g, u], [pa])
                    for c in range(8):
                        kb.op('pe', lambda e: e.matmul(pb[:, :], uu[:, c, :], u[:, c, :], start=(c == 0), stop=(c == 7)),
                              [uu, u], [pb])
                    s = rot(sg, f)
                    kb.op('act', lambda e: e.activation(s[:], pa[:, :], AF.Silu), [pa], [s])
                    kb.op('dve', lambda e: e.tensor_tensor(act[:, f, :], s[:], pb[:, :], ALU.mult), [s, pb], [act])
                if t + 1 < NTI:
                    self.norm_mod(rot(hts, t + 1), rot(us, t + 1), si, ((t + 1) * TT) // cfg.S, nt)
                for c in range(8):
                    wdt = rot(wd, c)
                    kb.dma('sp', wdt[:].rearrange("p f j -> p (f j)"), Wd.h[c], [Wd], [wdt], wdt)
                    pc = kb.ps[4 + c % 2]
                    for f in range(NFF):
                        kb.op('pe', lambda e: e.matmul(pc[:, :], wdt[:, f, :], act[:, f, :], start=(f == 0),
                                                       stop=(f == NFF - 1)), [wdt, act], [pc])
                    kb.op('dve', lambda e: e.scalar_tensor_tensor(ht[:, c, :], pc[:, :], self.tabG[:, si, c, b:b + 1],
                                                                  ht[:, c, :], ALU.mult, ALU.add),
                          [pc, self.tabG, ht], [ht])
                kb.dma('pool', self.hTr[t][:], ht[:], [ht], [self.hTr[t]], ht)


def build(cfg):
    P = Prog(cfg)
    P.xpose_in()
    for l in range(cfg.L):
        P.convert_weights(l)
        P.layer_setup(l)
        P.ffn(l, 0)
        if cfg.stop_after == 'ffn1':
            break
        P.mixer(l)
        if cfg.stop_after == 'mixer':
            break
        P.ffn(l, 1)
    P.xpose_out()
    P.kb.finish()
    return P


WNAMES = ['mod_w', 'ffn_w_gate', 'ffn_w_up', 'ffn_w_down', 'w_in', 'w_out_a', 'w_out_b', 'w_out_c', 'w_out',
          'rwkv_w2', 'rwkv_a2', 'rwkv_g2']


def run(cfg, inputs):
    inp = {k: np.asarray(v) for k, v in inputs.items()}
    NB, S, L = cfg.NB, cfg.S, cfg.L
    P = build(cfg)
    pv = np.stack([make_pv(inp, l, L) for l in range(L)], axis=0)
    shared = {n: np.ascontiguousarray(inp[n], dtype=np.float32) for n in WNAMES}
    shared['consts'] = CONSTS
    shared['pv'] = pv
    in_maps = []
    for k in range(cfg.NCORES):
        sl = slice(k * NB, (k + 1) * NB)
        m = dict(shared)
        m['x'] = np.ascontiguousarray(inp['x'][sl].reshape(NB * S, D), dtype=np.float32)
        c = np.asarray(inp['c'][sl], np.float32)
        m['cT'] = np.ascontiguousarray(c.reshape(NB, 8, 128).transpose(2, 1, 0))
        m['pos'] = np.ascontiguousarray(inp['positions'][sl], dtype=np.int32)
        in_maps.append(m)
    res = run_bass_kernel_spmd(P.nc, in_maps, core_ids=list(range(cfg.NCORES)))
    outs = [np.asarray(r['out']).reshape(NB, S, D) for r in res.results]
    return np.concatenate(outs, axis=0).astype(np.float32)


def kernel(**inputs):
    return run(Cfg(), inputs)


def _inproj(self, l):
    kb, cfg = self.kb, self.cfg
    Win = self.Win[l]
    with kb.phase():
        hts = [kb.tile("ht%d" % k, [128, 8, TT], F32) for k in range(2)]
        nt = self.norm_tiles()
        us = [kb.tile("u%d" % k, [128, 8, TT], BF16) for k in range(2)]
        wc = [kb.tile("wc%d" % k, [128, 8, 128], BF16) for k in range(4)]
        ot = [kb.tile("ot%d" % k, [128, TT], F32) for k in range(4)]
        vt = [kb.tile("vt%d" % k, [128, 512], BF16) for k in range(2)]
        wv = kb.tile("wv", [128, 4, 8, 128], BF16)
        kb.dma('sp', wv[:].rearrange("p f c j -> p f (c j)"), Win.h[8:12].rearrange("f p x -> p f x"), [Win], [wv], wv)
        n = 0
        NTI = cfg.NT // TT
        kb.dma('sp', hts[0][:], self.hTr[0][:], [self.hTr[0]], [hts[0]], hts[0])
        self.norm_mod(hts[0], us[0], 1, 0, nt)
        for t in range(NTI):
            b = (t * TT) // cfg.S
            ht, u = rot(hts, t), rot(us, t)
            if t + 1 < NTI:
                hn = rot(hts, t + 1)
                kb.dma('sp', hn[:], self.hTr[t + 1][:], [self.hTr[t + 1]], [hn], hn)
            for ch in range(NCH_IN):
                if 8 <= ch < 12:
                    continue
                w = rot(wc, n)
                o = rot(ot, n)
                ps = kb.ps[n % 4]
                n += 1
                M = 32 if ch == 26 else 128
                kb.dma('sp', w[:].rearrange("p c j -> p (c j)"), Win.h[ch], [Win], [w], w)
                for c in range(8):
                    kb.op('pe', lambda e: e.matmul(ps[0:M, :], w[:, c, 0:M], u[:, c, :], start=(c == 0), stop=(c == 7)),
                          [w, u], [ps])
                if 27 <= ch < 31 or 39 <= ch < 43:
                    kb.op('act', lambda e: e.activation(o[0:M, :], ps[0:M, :], AF.Silu), [ps], [o])
                elif ch >= 43:
                    kb.op('act', lambda e: e.activation(o[0:M, :], ps[0:M, :], AF.Sigmoid), [ps], [o])
                elif n % 2 == 0:
                    kb.op('act', lambda e: e.copy(o[0:M, :], ps[0:M, :]), [ps], [o])
                else:
                    kb.op('dve', lambda e: e.tensor_copy(o[0:M, :], ps[0:M, :]), [ps], [o])
                kb.dma('pool', self.pTr[ch][0:M, t * TT:(t + 1) * TT], o[0:M, :], [o], [self.pTr[ch]], o)
            if t + 1 < NTI:
                self.norm_mod(rot(hts, t + 1), rot(us, t + 1), 1, ((t + 1) * TT) // cfg.S, nt)
            for tb in range(TT // 128):
                psv = kb.ps[4 + tb % 2]
                v = rot(vt, tb)
                for vc in range(4):
                    for c in range(8):
                        kb.op('pe', lambda e: e.matmul(psv[:, vc * 128:(vc + 1) * 128], u[:, c, tb * 128:(tb + 1) * 128],
                                                       wv[:, vc, c, :], start=(c == 0), stop=(c == 7)), [u, wv], [psv])
                kb.op('dve', lambda e: e.tensor_copy(v[:], psv[:, :]), [psv], [v])
                r0 = t * TT + tb * 128
                kb.dma('pool', self.vtok.h[r0:r0 + 128, :], v[:], [v], [self.vtok], v)


Prog.inproj = _inproj


def _rsqrt(self, out, in_ap, in_bufs, scale, eps):
    kb = self.kb
    col = {EPS: 0, RWKV_LN_EPS: 1, 1e-24: 2}[eps]
    kb.op('act', lambda e: e.activation(out[:], in_ap, AF.Ln, bias=self.epst[:, col:col + 1], scale=scale),
          in_bufs + [self.epst], [out])
    kb.op('act', lambda e: e.activation(out[:], out[:], AF.Exp, scale=-0.5), [out], [out])


Prog.rsqrt = _rsqrt


def _attention(self, l):
    kb, cfg = self.kb, self.cfg
    S, NB = cfg.S, cfg.NB
    NG = S // TT
    lam_init = 0.8 - 0.6 * math.exp(-0.3 * l)
    PI = math.pi
    with kb.phase():
        posi = kb.tile("posi", [128, S], I32)
        ang = kb.tile("ang", [128, S], F32)
        rt = kb.tile("rrt", [128, S], F32)
        ri = kb.tile("rri", [128, S], I32)
        cs = kb.tile("cos", [128, S], F32)
        sn = kb.tile("sin", [128, S], F32)
        raw = [kb.tile("qkraw%d" % i, [128, S], F32) for i in range(2)]
        qk = [kb.tile("qkbf%d" % i, [128, S], BF16) for i in range(2)]
        V = kb.tile("V", [128, S // 128, 128], BF16)
        sqs = [kb.tile("asq%d" % i, [128, TT], BF16) for i in range(3)]
        rstds = [kb.tile("arstd%d" % i, [128, TT], F32) for i in range(3)]
        xns = [kb.tile("axn%d" % i, [128, TT], F32) for i in range(3)]
        t1s = [kb.tile("at1%d" % i, [128, TT], F32) for i in range(3)]
        t2s = [kb.tile("at2%d" % i, [128, TT], F32) for i in range(3)]
        sq, rstd = sqs[0], rstds[0]
        npre = 0
        pts = [kb.tile("pt%d" % i, [128, TT], BF16) for i in range(4)]
        rl = [kb.tile("rl%d" % i, [128, TT], F32) for i in range(2)]
        oo = [kb.tile("oo%d" % i, [128, TT], F32) for i in range(2)]
        dd = kb.tile("dd", [128, TT], F32)
        ob = [kb.tile("ob%d" % i, [128, TT], BF16) for i in range(2)]
        sw = kb.tile("sw", [128, 1], F32)
        kb.op('dve', lambda e: e.tensor_scalar(sw[:], self.P('subln'), 1.0 - lam_init, None, ALU.mult), [self.pvt], [sw])
        n = 0
        for b in range(NB):
            kb.dma('sp', posi[:], self.pos.h[b:b + 1, :].partition_broadcast(128), [self.pos], [posi], posi)
            kb.op('dve', lambda e: e.tensor_copy(ang[:], posi[:]), [posi], [ang])
            kb.op('dve', lambda e: e.tensor_scalar(ang[:], ang[:], self.C('invf'), None, ALU.mult), [ang, self.cst], [ang])
            for (dst, shift) in ((sn, 0.0), (cs, 0.5 * PI)):
                C1 = 6.28125
                C2 = 2 * PI - C1
                kb.op('dve', lambda e: e.tensor_scalar(dst[:], ang[:], shift, None, ALU.add), [ang], [dst])
                kb.op('dve', lambda e: e.tensor_scalar(rt[:], dst[:], 1.0 / (2 * PI), None, ALU.mult), [dst], [rt])
                kb.op('dve', lambda e: e.tensor_copy(ri[:], rt[:]), [rt], [ri])
                kb.op('dve', lambda e: e.tensor_copy(rt[:], ri[:]), [ri], [rt])
                kb.op('dve', lambda e: e.scalar_tensor_tensor(dst[:], rt[:], -C1, dst[:], ALU.mult, ALU.add), [rt, dst], [dst])
                kb.op('dve', lambda e: e.scalar_tensor_tensor(dst[:], rt[:], -C2, dst[:], ALU.mult, ALU.add), [rt, dst], [dst])
                kb.op('dve', lambda e: e.tensor_scalar(rt[:], dst[:], PI, 2 * PI, ALU.is_gt, ALU.mult), [dst], [rt])
                kb.op('dve', lambda e: e.tensor_tensor(dst[:], dst[:], rt[:], ALU.subtract), [dst, rt], [dst])
                kb.op('dve', lambda e: e.tensor_scalar(dst[:], dst[:], -PI, PI, ALU.max, ALU.min), [dst], [dst])
                kb.op('act', lambda e: e.activation(dst[:], dst[:], AF.Sin), [dst], [dst])
            for h in range(4):
                tok = slice(b * S, (b + 1) * S)
                for j in range(2):
                    kb.dma('sp', raw[j][:], self.pTr[4 * j + h][:, tok], [self.pTr[4 * j + h]], [raw[j]], raw[j])
                kb.dma('sp', V[:], self.vtok.h[tok, h * 128:(h + 1) * 128].rearrange("(j p) e -> p j e", p=128),
                       [self.vtok], [V], V)
                for j in range(2):
                    for g in range(NG):
                        cs_ = slice(g * TT, (g + 1) * TT)
                        p6, p7 = kb.ps[(2 * npre) % 8], kb.ps[(2 * npre + 1) % 8]
                        sq, rstd, xn, t1, t2 = (rot(x_, npre) for x_ in (sqs, rstds, xns, t1s, t2s))
                        npre += 1
                        kb.op('act', lambda e: e.activation(sq[:], raw[j][:, cs_], AF.Square), [raw[j]], [sq])
                        kb.op('pe', lambda e: e.matmul(p6[:, :], self.bones_bf[:], sq[:], start=True, stop=True),
                              [self.bones_bf, sq], [p6])
                        self.rsqrt(rstd, p6[:, :], [p6], 1.0 / 64, EPS)
                        kb.op('dve', lambda e: e.scalar_tensor_tensor(xn[:], raw[j][:, cs_], self.P('qkw', j), rstd[:],
                                                                      ALU.mult, ALU.mult), [raw[j], self.pvt, rstd], [xn])
                        kb.op('pe', lambda e: e.matmul(p7[:, :], self.C('rot'), xn[:], start=True, stop=True),
                              [self.cst, xn], [p7])
                        kb.op('dve', lambda e: e.tensor_tensor(t1[:], xn[:], cs[:, cs_], ALU.mult), [xn, cs], [t1])
                        kb.op('dve', lambda e: e.tensor_tensor(t2[:], p7[:, :], sn[:, cs_], ALU.mult), [p7, sn], [t2])
                        kb.op('dve', lambda e: e.tensor_tensor(qk[j][:, cs_], t1[:], t2[:], ALU.add), [t1, t2], [qk[j]])
                q, k = qk
                pO = [kb.ps[2], kb.ps[3]]
                pL = [kb.ps[4], kb.ps[5]]
                its = [(G, m, j) for G in range(NG) for m in range(2) for j in range(4 * G + 4)]

                def emit_s(idx):
                    G, m, j = its[idx]
                    rows = slice(m * 64, (m + 1) * 64)
                    c0 = max(0, j - 4 * G) * 128
                    pS = kb.ps[idx % 2]
                    kb.op('pe', lambda e: e.matmul(pS[:, c0:TT], k[rows, j * 128:(j + 1) * 128],
                                                   q[rows, G * TT + c0:(G + 1) * TT], start=True, stop=True), [k, q], [pS])

                emit_s(0)
                for idx, (G, m, j) in enumerate(its):
                    nj = 4 * G + 4
                    r = j - 4 * G
                    c0 = max(0, r) * 128
                    pS = kb.ps[idx % 2]
                    pt = rot(pts, idx)
                    kb.op('act', lambda e: e.activation(pt[:, c0:TT], pS[:, c0:TT], AF.Exp, scale=0.125), [pS], [pt])
                    if idx + 1 < len(its):
                        emit_s(idx + 1)
                    if r >= 0:
                        kb.op('pool', lambda e: e.tensor_tensor(pt[:, c0:c0 + 128], pt[:, c0:c0 + 128],
                                                                self.maskA_bf[:], ALU.mult), [pt, self.maskA_bf], [pt])
                    kb.op('pe', lambda e: e.matmul(pO[m][:, c0:TT], V[:, j, :], pt[:, c0:TT], start=(j == 0),
                                                   stop=(j == nj - 1)), [V, pt], [pO[m]])
                    kb.op('pe', lambda e: e.matmul(pL[m][:, c0:TT], self.ones_bf[:], pt[:, c0:TT], start=(j == 0),
                                                   stop=(j == nj - 1)), [self.ones_bf, pt], [pL[m]])
                    if j != nj - 1:
                        continue
                    kb.op('act', lambda e: e.activation(rl[m][:], pL[m][:, :], AF.Ln), [pL[m]], [rl[m]])
                    kb.op('act', lambda e: e.activation(rl[m][:], rl[m][:], AF.Exp, scale=-1.0), [rl[m]], [rl[m]])
                    kb.op('dve', lambda e: e.tensor_tensor(oo[m][:], pO[m][:, :], rl[m][:], ALU.mult), [pO[m], rl[m]], [oo[m]])
                    if m != 1:
                        continue
                    kb.op('dve', lambda e: e.scalar_tensor_tensor(dd[:], oo[1][:], self.lam[:, 1:2], oo[0][:], ALU.mult, ALU.add),
                          [oo[1], oo[0], self.lam], [dd])
                    p6 = kb.ps[6 + G % 2]
                    sq, rstd = rot(sqs, G), rot(rstds, G)
                    kb.op('act', lambda e: e.activation(sq[:], dd[:], AF.Square), [dd], [sq])
                    kb.op('pe', lambda e: e.matmul(p6[:, :], self.ones_bf[:], sq[:], start=True, stop=True),
                          [self.ones_bf, sq], [p6])
                    self.rsqrt(rstd, p6[:, :], [p6], 1.0 / 128, EPS)
                    o = rot(ob, G)
                    kb.op('dve', lambda e: e.scalar_tensor_tensor(o[:], dd[:], sw[:, 0:1], rstd[:], ALU.mult, ALU.mult),
                          [dd, sw, rstd], [o])
                    kb.dma('pool', self.oT[0].h[h, :, b * S + G * TT:b * S + (G + 1) * TT], o[:], [o], [self.oT[0]], o)


Prog.attention = _attention


def _merge(self, l, branches=(0, 1, 2)):
    kb, cfg = self.kb, self.cfg
    with kb.phase():
        hts = [kb.tile("ht%d" % k, [128, 8, TT], F32) for k in range(2)]
        wo = [kb.tile("wo%d" % i, [128, 8, 4, 128], BF16) for i in range(3)]
        woo = kb.tile("woo", [128, 8, 8, 128], BF16)
        for i in branches:
            kb.dma('sp', wo[i][:].rearrange("p f k j -> p f (k j)"), self.Wo[l][i].h.rearrange("f p x -> p f x"),
                   [self.Wo[l][i]], [wo[i]], wo[i])
        kb.dma('sp', woo[:].rearrange("p f k j -> p f (k j)"), self.Woo[l].h.rearrange("f p x -> p f x"),
               [self.Woo[l]], [woo], woo)
        ob = [[kb.tile("mo%d_%d" % (i, k), [128, 4, TT], BF16) for k in range(2)] for i in range(3)]
        gt = [kb.tile("mg%d" % k, [128, TT], F32) for k in range(6)]
        tm = [kb.tile("mt%d" % k, [128, TT], F32) for k in range(3)]
        z = kb.tile("mz", [128, 8, TT], BF16)
        n = 0
        for t in range(cfg.NT // TT):
            b = (t * TT) // cfg.S
            tok = slice(t * TT, (t + 1) * TT)
            ht = rot(hts, t)
            kb.dma('sp', ht[:], self.hTr[t][:], [self.hTr[t]], [ht], ht)
            o = {}
            for i in branches:
                o[i] = rot(ob[i], t)
                kb.dma('sp', o[i][:], self.oT[i].h[:, :, tok].rearrange("f p t -> p f t"), [self.oT[i]], [o[i]], o[i])
            for f in range(8):
                first = True
                for i in branches:
                    ps = kb.ps[(n % 2) * 3 + i]
                    g = rot(gt, n * 3 + i)
                    ch = 43 + 8 * i + f
                    kb.dma('sp', g[:], self.pTr[ch][:, tok], [self.pTr[ch]], [g], g)
                    for kc in range(4):
                        kb.op('pe', lambda e: e.matmul(ps[:, :], wo[i][:, f, kc, :], o[i][:, kc, :], start=(kc == 0),
                                                       stop=(kc == 3)), [wo[i], o[i]], [ps])
                    if len(branches) == 1:
                        kb.op('dve', lambda e: e.tensor_tensor(z[:, f, :], ps[:, :], g[:], ALU.mult), [ps, g], [z])
                    elif first:
                        acc = rot(tm, n)
                        kb.op('dve', lambda e: e.tensor_tensor(acc[:], ps[:, :], g[:], ALU.mult), [ps, g], [acc])
                    else:
                        kb.op('dve', lambda e: e.tensor_tensor(g[:], ps[:, :], g[:], ALU.mult), [ps, g], [g])
                        last = (i == branches[-1])
                        if last:
                            kb.op('pool', lambda e: e.tensor_tensor(z[:, f, :], acc[:], g[:], ALU.add), [acc, g], [z])
                        else:
                            kb.op('pool', lambda e: e.tensor_tensor(acc[:], acc[:], g[:], ALU.add), [acc, g], [acc])
                    first = False
                n += 1
            for f in range(8):
                ps = kb.ps[6 + f % 2]
                for kc in range(8):
                    kb.op('pe', lambda e: e.matmul(ps[:, :], woo[:, f, kc, :], z[:, kc, :], start=(kc == 0), stop=(kc == 7)),
                          [woo, z], [ps])
                kb.op('dve', lambda e: e.scalar_tensor_tensor(ht[:, f, :], ps[:, :], self.tabG[:, 1, f, b:b + 1], ht[:, f, :],
                                                              ALU.mult, ALU.add), [ps, self.tabG, ht], [ht])
            kb.dma('pool', self.hTr[t][:], ht[:], [ht], [self.hTr[t]], ht)


Prog.merge = _merge


def _mixer(self, l):
    br = self.cfg.branches
    self.inproj(l)
    if 0 in br:
        self.attention(l)
    if 1 in br:
        self.rwkv(l)
    if 2 in br:
        self.hgrn(l)
    self.merge(l, br)


Prog.mixer = _mixer


def _hgrn(self, l):
    kb, cfg = self.kb, self.cfg
    S, NB = cfg.S, cfg.NB
    NTL = S // TT
    NCK = TT // CH
    with kb.phase():
        def mk(name, shape, dt, nbuf):
            return [[kb.tile("%s%d_%d" % (name, h, k), shape, dt) for k in range(nbuf)] for h in range(4)]
        qr, fz, iv, gg = (mk(nm, [128, TT], F32, 2) for nm in ("hq", "hf", "hi", "hg"))
        logf, kx, cc, qt, kt, kp = (mk(nm, [128, TT], F32, 1) for nm in ("hlf", "hkx", "hc", "hqt", "hkt", "hkp"))
        WE4 = kb.tile("hWE4", [128, 4, NCK], F32)
        St = [kb.tile("hS%d" % k, [128, 4, 128], F32) for k in range(2)]
        tmpS = kb.tile("htmpS", [128, 4, 128], F32)
        AM = [kb.tile("hAM%d" % k, [32, 128], F32) for k in range(2)]
        VT = [kb.tile("hVT%d" % k, [32, 512], F32) for k in range(2)]
        KT = [kb.tile("hKT%d" % k, [32, 512], F32) for k in range(2)]
        osb = kb.tile("hosb", [128, TT], F32)
        OSB = kb.tile("hOSB", [128, 4, TT], F32)
        sq = kb.tile("hsq", [128, TT], BF16)
        rstd = kb.tile("hrstd", [128, TT], F32)
        yb = [kb.tile("hyb%d" % k, [128, TT], BF16) for k in range(2)]
        RA = [kb.ps[0]] * 4
        RV = [kb.ps[1]] * 4
        RK = [kb.ps[2]] * 4
        RS = [kb.ps[3]] * 4
        pO = [kb.ps[4 + h] for h in range(4)]
        ident = self.C('ident')
        nn = 0
        for b in range(NB):
            par = 0
            kb.op('pool', lambda e: e.memset(St[0][:], 0.0), [], [St[0]])
            for tl in range(NTL):
                t = b * NTL + tl
                tok = slice(t * TT, (t + 1) * TT)
                k2 = t % 2
                for h in range(4):
                    for (dst, ch) in ((qr, 27), (fz, 31), (iv, 35), (gg, 39)):
                        d = dst[h][k2]
                        kb.dma('sp', d[:], self.pTr[ch + h][:, tok], [self.pTr[ch + h]], [d], d)
                for h in range(4):
                    lf, kxx, c, q_, k_, kp_ = logf[h][0], kx[h][0], cc[h][0], qt[h][0], kt[h][0], kp[h][0]
                    kb.op('act', lambda e: e.activation(lf[:], fz[h][k2][:], AF.Sigmoid), [fz[h][k2]], [lf])
                    kb.op('dve', lambda e: e.tensor_scalar(lf[:], lf[:], self.lb[:, 4 + h:5 + h], self.lb[:, h:h + 1],
                                                           ALU.mult, ALU.add), [lf, self.lb], [lf])
                    kb.op('dve', lambda e: e.tensor_scalar(kxx[:], lf[:], -1.0, 1.0, ALU.mult, ALU.add), [lf], [kxx])
                    kb.op('act', lambda e: e.activation(lf[:], lf[:], AF.Ln), [lf], [lf])
                    kb.op('dve', lambda e: e.tensor_tensor_scan(c[:], self.C('r32'), lf[:], 0.0, ALU.mult, ALU.add),
                          [self.cst, lf], [c])
                    kb.op('act', lambda e: e.activation(q_[:], c[:], AF.Exp), [c], [q_])
                    kb.op('dve', lambda e: e.tensor_tensor(q_[:], q_[:], qr[h][k2][:], ALU.mult), [q_, qr[h][k2]], [q_])
                    kb.op('act', lambda e: e.activation(k_[:], c[:], AF.Exp, scale=-1.0), [c], [k_])
                    kb.op('dve', lambda e: e.tensor_tensor(k_[:], k_[:], kxx[:], ALU.mult), [k_, kxx], [k_])
                    kb.op('act', lambda e: e.activation(WE4[:, h, :], c[:, CH - 1::CH], AF.Exp), [c], [WE4])
                    kb.op('dve', lambda e: e.tensor_tensor(kp_[:].rearrange("p (n j) -> p n j", j=CH),
                                                           k_[:].rearrange("p (n j) -> p n j", j=CH),
                                                           WE4[:, h, :].unsqueeze(2).to_broadcast([128, NCK, CH]), ALU.mult),
                          [k_, WE4], [kp_])
                def front(ci):
                    cs_ = slice(ci * CH, (ci + 1) * CH)
                    pa_ = ci % 2
                    bA, bV, bK = kb.ps[0 + pa_], kb.ps[2 + pa_], kb.ps[4 + pa_]
                    am_, vt_, kt_ = AM[pa_], VT[pa_], KT[pa_]
                    for h in range(4):
                        kb.op('pe', lambda e: e.matmul(bA[0:32, h * 32:(h + 1) * 32], kt[h][0][:, cs_], qt[h][0][:, cs_],
                                                       start=True, stop=True), [kt[h][0], qt[h][0]], [bA])
                    for h in range(4):
                        kb.op('pe', lambda e: e.matmul(bV[0:32, h * 128:(h + 1) * 128], iv[h][k2][:, cs_], ident,
                                                       start=True, stop=True), [iv[h][k2], self.cst], [bV])
                    for h in range(4):
                        kb.op('pe', lambda e: e.matmul(bK[0:32, h * 128:(h + 1) * 128], kp[h][0][:, cs_], ident,
                                                       start=True, stop=True), [kp[h][0], self.cst], [bK])
                    kb.op('dve', lambda e: e.tensor_tensor(am_[:], bA[0:32, 0:128], self.C('m_h4', slice(0, 32)), ALU.mult),
                          [bA, self.cst], [am_])
                    kb.op('act', lambda e: e.copy(vt_[:], bV[0:32, :]), [bV], [vt_])
                    kb.op('act', lambda e: e.copy(kt_[:], bK[0:32, :]), [bK], [kt_])

                front(0)
                for ci in range(NCK):
                    cs_ = slice(ci * CH, (ci + 1) * CH)
                    cur, nxt = St[par], St[1 - par]
                    pa_ = ci % 2
                    am_, vt_, kt_ = AM[pa_], VT[pa_], KT[pa_]
                    bS, bO = kb.ps[6], kb.ps[7]
                    if ci + 1 < NCK:
                        front(ci + 1)
                    for h in range(4):
                        hs = slice(h * 128, (h + 1) * 128)
                        oc = slice(h * CH, (h + 1) * CH)
                        kb.op('pe', lambda e: e.matmul(bO[:, oc], cur[:, h, :], qt[h][0][:, cs_], start=True, stop=False),
                              [cur, qt[h][0]], [bO])
                        kb.op('pe', lambda e: e.matmul(bO[:, oc], vt_[:, hs], am_[:, h * 32:(h + 1) * 32], start=False, stop=True),
                              [vt_, am_], [bO])
                        kb.op('pe', lambda e: e.matmul(bS[:, hs], kt_[:, hs], vt_[:, hs], start=True, stop=True),
                              [kt_, vt_], [bS])
                    kb.op('dve', lambda e: e.tensor_tensor(tmpS[:], cur[:], WE4[:, :, ci:ci + 1].to_broadcast([128, 4, 128]),
                                                           ALU.mult), [cur, WE4], [tmpS])
                    kb.op('dve', lambda e: e.tensor_tensor(nxt[:], tmpS[:], bS[:, :].rearrange("p (a v) -> p a v", v=128), ALU.add),
                          [tmpS, bS], [nxt])
                    kb.op('act', lambda e: e.copy(OSB[:, :, cs_], bO[:, 0:4 * CH].rearrange("p (a v) -> p a v", v=CH)), [bO], [OSB])
                    par ^= 1
                    nn += 1
                for h in range(4):
                    regs = [kb.ps[h]]
                    kb.op('act', lambda e: e.activation(sq[:], OSB[:, h, :], AF.Square), [OSB], [sq])
                    kb.op('pe', lambda e: e.matmul(kb.ps[h][:, :], self.ones_bf[:], sq[:], start=True, stop=True),
                          [self.ones_bf, sq], regs)
                    self.rsqrt(rstd, kb.ps[h][:, :], regs, 1.0 / 128, EPS)
                    kb.op('dve', lambda e: e.scalar_tensor_tensor(osb[:], OSB[:, h, :], self.P('hnw'), rstd[:], ALU.mult, ALU.mult),
                          [OSB, self.pvt, rstd], [osb])
                    y = rot(yb, h)
                    kb.op('pool', lambda e: e.tensor_tensor(y[:], osb[:], gg[h][k2][:], ALU.mult), [osb, gg[h][k2]], [y])
                    kb.dma('pool', self.oT[2].h[h, :, tok], y[:], [y], [self.oT[2]], y)


Prog.hgrn = _hgrn


def _rwkv(self, l):
    kb, cfg = self.kb, self.cfg
    S, NB = cfg.S, cfg.NB
    NTL = S // TT
    NCK = TT // CB
    W = self.w
    with kb.phase():
        F = lambda name, shape=(128, TT), dt=F32: kb.tile(name, list(shape), dt)
        w2sb, a2sb, g2a, g2b = F("w2sb", (128, 512)), F("a2sb", (128, 512)), F("g2a", (128, 512)), F("g2b", (32, 512))
        kb.dma('sp', w2sb[0:64, :], W['rwkv_w2'].h[l], [W['rwkv_w2']], [w2sb], w2sb)
        kb.dma('sp', a2sb[64:128, :], W['rwkv_a2'].h[l], [W['rwkv_a2']], [a2sb], a2sb)
        kb.dma('sp', g2a[:], W['rwkv_g2'].h[l, 0:128, :], [W['rwkv_g2']], [g2a], g2a)
        kb.dma('sp', g2b[:], W['rwkv_g2'].h[l, 128:160, :], [W['rwkv_g2']], [g2b], g2b)
        omka = F("omka", (128, 4))
        kb.op('dve', lambda e: e.tensor_scalar(omka[:], self.P('k_a'), -1.0, 1.0, ALU.mult, ALU.add), [self.pvt], [omka])
        raws = [[F("rraw%d_%d" % (j, k), (128, TT + 1)) for k in range(2)] for j in range(3)]
        lraw = [raws[j][0] for j in range(3)]
        xs = [F("lxs%d" % j) for j in range(3)]
        TH, SG1, SG2 = F("TH"), F("SG1"), F("SG2")
        LWt, AAt, G = F("LWt"), F("AAt"), F("G", (128, 4, TT))
        AL, BE, KA, RH, VV, BN = ([F("%s%d" % (nm, hp)) for hp in range(4)]
                                  for nm in ("AL", "BE", "KA", "RH", "VV", "BN"))
        WE = F("WE", (128, 4, NCK))
        t_r, t_k, tm1, tm2, tm3, tm4, tc = (F(nm) for nm in ("t_r", "t_k", "tm1", "tm2", "tm3", "tm4", "tc"))
        dtmp = F("dtmp")
        St = [F("St%d" % k, (128, 4, 64)) for k in range(2)]
        tmpS = F("tmpS", (128, 4, 64))
        BKV = [[F("BKV%d_%d" % (hp, k), (128, 192)) for k in range(2)] for hp in range(4)]
        NMM = [[F("NMM%d_%d" % (hp, k), (128, 192)) for k in range(2)] for hp in range(4)]
        TTt = [[F("Tt%d_%d" % (hp, k), (128, 128)) for k in range(7)] for hp in range(4)]
        P1 = [F("P1_%d" % hp, (128, 256)) for hp in range(4)]
        PP = [[F("PP%d_%d" % (hp, k), (128, 256)) for k in range(2)] for hp in range(4)]
        XS, US = F("XS", (128, 256)), F("US", (128, 256))
        OSB = F("OSB", (128, 4, TT))
        cen, sqv, rstd = tm1, tm2, tm4
        yb = [F("ryb%d" % k, (128, TT), BF16) for k in range(2)]
        for hp in range(4):
            kb.op('pool', lambda e: e.memset(P1[hp][:], 0.0), [], [P1[hp]])
        b0, b1, b2, b3, b4, b5, b6, b7 = kb.ps
        io = CC['ident'][0]
        mask3 = self.cst[:, CC['m_lt'][0]:CC['m_lt'][0] + 192]
        bones = self.C('bones64')
        NEGE = -math.exp(-0.5)
        tt_idx = [0] * 4
        import os
        DBG = int(os.environ.get('RWKV_DBG', '0'))

        def shift_lerp(dst_ap, dst, raw, mucol, rows=slice(0, 128)):
            kb.op('dve', lambda e: e.tensor_tensor(dtmp[rows, :], raw[rows, 0:TT], raw[rows, 1:TT + 1], ALU.subtract),
                  [raw], [dtmp])
            kb.op('dve', lambda e: e.scalar_tensor_tensor(dst_ap, dtmp[rows, :], mucol, raw[rows, 1:TT + 1], ALU.mult, ALU.add),
                  [dtmp, raw, self.pvt], [dst])

        def load_halo(raw, ch, t, tl, rows=slice(0, 128)):
            if tl == 0:
                kb.op('pool', lambda e: e.memset(raw[rows, 0:1], 0.0), [], [raw])
                kb.dma('sp', raw[rows, 1:TT + 1], self.pTr[ch][rows, t * TT:(t + 1) * TT], [self.pTr[ch]], [raw], raw)
            else:
                kb.dma('sp', raw[rows, :], self.pTr[ch][rows, t * TT - 1:(t + 1) * TT], [self.pTr[ch]], [raw], raw)

        def stage_a(ci, hps, k2, out):
            cs_ = slice(ci * CB, (ci + 1) * CB)
            for hp in hps:
                bk = kb.ps[hp]
                for hh in range(2):
                    rows = slice(hh * 64, hh * 64 + 64)
                    idr = self.cst[rows, io + hh * 64:io + hh * 64 + 64]
                    for jj, src in enumerate((BE, KA, VV)):
                        kb.op('pe', lambda e: e.matmul(bk[rows, jj * 64:(jj + 1) * 64], src[hp][rows, cs_], idr,
                                                       start=True, stop=True), [src[hp], self.cst], [bk])
                    for jj, (lh, rh) in enumerate(((AL, BE), (BE, AL), (KA, AL), (BE, RH), (KA, RH))):
                        kb.op('pe', lambda e: e.matmul(bk[rows, 192 + jj * 64:192 + (jj + 1) * 64], lh[hp][rows, cs_],
                                                       rh[hp][rows, cs_], start=True, stop=True), [lh[hp], rh[hp]], [bk])
            yield
            for hp in hps:
                bk = kb.ps[hp]
                kb.op('act', lambda e: e.copy(BKV[hp][k2][:], bk[:, 0:192]), [bk], [BKV[hp][k2]])
                for hh in range(2):
                    rows = slice(hh * 64, hh * 64 + 64)
                    kb.op('dve', lambda e: e.tensor_tensor(P1[hp][rows, hh * 64:hh * 64 + 64], bk[rows, 192:256],
                                                           self.C('m_sl', rows), ALU.mult), [bk, self.cst], [P1[hp]])
                    kb.op('dve', lambda e: e.tensor_tensor(P1[hp][rows, 128 + hh * 64:128 + hh * 64 + 64], bk[rows, 256:320],
                                                           self.C('m_lt', rows), ALU.mult), [bk, self.cst], [P1[hp]])
                kb.op('dve', lambda e: e.tensor_tensor(NMM[hp][k2][:], bk[:, 320:512], mask3, ALU.mult), [bk, self.cst],
                      [NMM[hp][k2]])
            yield
            cur = {}
            for hp in hps:
                tt = TTt[hp][tt_idx[hp] % 7]
                tt_idx[hp] += 1
                kb.op('pool', lambda e: e.tensor_tensor(tt[:], self.C('ident'), P1[hp][:, 128:256], ALU.add),
                      [self.cst, P1[hp]], [tt])
                cur[hp] = (P1[hp], tt)
            for rnd in range(6):
                for hp in hps:
                    bk = kb.ps[hp]
                    pp, tt = cur[hp]
                    P_, Pt_ = pp[:, 0:128], pp[:, 128:256]
                    if rnd >= 1:
                        kb.op('pe', lambda e: e.matmul(bk[:, 256:384], P_, tt[:], start=True, stop=True), [pp, tt], [bk])
                    if rnd <= 4:
                        kb.op('pe', lambda e: e.matmul(bk[:, 0:128], Pt_, P_, start=True, stop=True), [pp], [bk])
                    if rnd <= 3:
                        kb.op('pe', lambda e: e.matmul(bk[:, 128:256], P_, Pt_, start=True, stop=True), [pp], [bk])
                yield
                for hp in hps:
                    bk = kb.ps[hp]
                    pp, tt = cur[hp]
                    npp, ntt = pp, tt
                    if rnd <= 4:
                        npp = PP[hp][rnd % 2]
                        wdt = 256 if rnd <= 3 else 128
                        kb.op('act', lambda e: e.copy(npp[:, 0:wdt], bk[:, 0:wdt]), [bk], [npp])
                    if rnd >= 1:
                        ntt = TTt[hp][tt_idx[hp] % 7]
                        tt_idx[hp] += 1
                        kb.op('dve', lambda e: e.tensor_tensor(ntt[:], tt[:], bk[:, 256:384], ALU.add), [tt, bk], [ntt])
                    cur[hp] = (npp, ntt)
                yield
            for hp in hps:
                out[hp] = cur[hp][1]

        def stage_b(ci, k2, tts, par):
            cs_ = slice(ci * CB, (ci + 1) * CB)
            Sc, Sn = St[par], St[1 - par]
            for hp in range(4):
                hc = slice(hp * 64, hp * 64 + 64)
                for hh in range(2):
                    rows = slice(hh * 64, hh * 64 + 64)
                    kb.op('pe', lambda e: e.matmul(b4[rows, hc], AL[hp][rows, cs_], Sc[rows, hp, :], start=True, stop=False),
                          [AL[hp], Sc], [b4])
                    kb.op('pe', lambda e: e.matmul(b4[rows, hc], NMM[hp][k2][rows, 0:64], BKV[hp][k2][rows, 128:192],
                                                   start=False, stop=True), [NMM[hp][k2], BKV[hp][k2]], [b4])
            yield
            kb.op('act', lambda e: e.copy(XS[:], b4[:, 0:256]), [b4], [XS])
            yield
            for hp in range(4):
                hc = slice(hp * 64, hp * 64 + 64)
                kb.op('pe', lambda e: e.matmul(b5[:, hc], tts[hp][:], XS[:, hc], start=True, stop=True), [tts[hp], XS], [b5])
            yield
            kb.op('dve', lambda e: e.tensor_copy(US[:], b5[:, 0:256]), [b5], [US])
            yield
            for hp in range(4):
                hc = slice(hp * 64, hp * 64 + 64)
                for hh in range(2):
                    rows = slice(hh * 64, hh * 64 + 64)
                    kb.op('pe', lambda e: e.matmul(b6[rows, hc], BKV[hp][k2][rows, 0:64], US[rows, hc], start=True, stop=False),
                          [BKV[hp][k2], US], [b6])
                    kb.op('pe', lambda e: e.matmul(b6[rows, hc], BKV[hp][k2][rows, 64:128], BKV[hp][k2][rows, 128:192],
                                                   start=False, stop=True), [BKV[hp][k2]], [b6])
                    kb.op('pe', lambda e: e.matmul(b7[rows, hc], Sc[rows, hp, :], RH[hp][rows, cs_], start=True, stop=False),
                          [Sc, RH[hp]], [b7])
                    kb.op('pe', lambda e: e.matmul(b7[rows, hc], US[rows, hc], NMM[hp][k2][rows, 64:128], start=False, stop=False),
                          [US, NMM[hp][k2]], [b7])
                    kb.op('pe', lambda e: e.matmul(b7[rows, hc], BKV[hp][k2][rows, 128:192], NMM[hp][k2][rows, 128:192],
                                                   start=False, stop=True), [BKV[hp][k2], NMM[hp][k2]], [b7])
            yield
            kb.op('dve', lambda e: e.tensor_tensor(tmpS[:], Sc[:], b6[:, 0:256].rearrange("p (a v) -> p a v", v=64), ALU.add),
                  [Sc, b6], [tmpS])
            kb.op('dve', lambda e: e.tensor_tensor(Sn[:], tmpS[:], WE[:, :, ci:ci + 1].to_broadcast([128, 4, 64]), ALU.mult),
                  [tmpS, WE], [Sn])
            kb.op('act', lambda e: e.copy(OSB[:, :, cs_], b7[:, 0:256].rearrange("p (a v) -> p a v", v=64)), [b7], [OSB])

        for b in range(NB):
            par = 0
            kb.op('pool', lambda e: e.memset(St[0][:], 0.0), [], [St[0]])
            for tl in range(NTL):
                t = b * NTL + tl
                tok = slice(t * TT, (t + 1) * TT)
                load_halo(lraw[0], 24, t, tl)
                load_halo(lraw[1], 25, t, tl)
                load_halo(lraw[2], 26, t, tl, slice(0, 32))
                shift_lerp(xs[0][:], xs[0], lraw[0], self.P('mu_wa'))
                shift_lerp(xs[1][:], xs[1], lraw[1], self.P('mu_g1'))
                shift_lerp(xs[2][0:32, :], xs[2], lraw[2], self.P('mu_g2', None, slice(0, 32)), slice(0, 32))
                kb.op('act', lambda e: e.activation(TH[0:64, :], xs[0][0:64, :], AF.Tanh), [xs[0]], [TH])
                kb.op('act', lambda e: e.activation(SG1[:], xs[1][:], AF.Sigmoid), [xs[1]], [SG1])
                kb.op('act', lambda e: e.activation(SG2[0:32, :], xs[2][0:32, :], AF.Sigmoid), [xs[2]], [SG2])
                for hp in range(4):
                    k2_ = hp % 2
                    rr, rk_, rv = raws[0][k2_], raws[1][k2_], raws[2][k2_]
                    load_halo(rr, 12 + hp, t, tl)
                    load_halo(rk_, 16 + hp, t, tl)
                    load_halo(rv, 20 + hp, t, tl)
                    shift_lerp(t_r[:], t_r, rr, self.P('mu', hp))
                    shift_lerp(t_k[:], t_k, rk_, self.P('mu', 4 + hp))
                    shift_lerp(VV[hp][:], VV[hp], rv, self.P('mu', 8 + hp))
                    fc = hp
                    fcs = slice(fc * 128, (fc + 1) * 128)
                    kb.op('pe', lambda e: e.matmul(b4[:, :], w2sb[0:64, fcs], TH[0:64, :], start=True, stop=True), [w2sb, TH], [b4])
                    kb.op('act', lambda e: e.activation(LWt[:], b4[:, :], AF.Sigmoid, bias=self.P('w0', fc), scale=1.0),
                          [b4, self.pvt], [LWt])
                    kb.op('pe', lambda e: e.matmul(b5[:, :], a2sb[64:128, fcs], xs[0][64:128, :], start=True, stop=True),
                          [a2sb, xs[0]], [b5])
                    kb.op('act', lambda e: e.activation(AAt[:], b5[:, :], AF.Sigmoid, bias=self.P('a0', fc), scale=1.0),
                          [b5, self.pvt], [AAt])
                    kb.op('pe', lambda e: e.matmul(b6[:, :], g2a[:, fcs], SG1[:], start=True, stop=False), [g2a, SG1], [b6])
                    kb.op('pe', lambda e: e.matmul(b6[:, :], g2b[0:32, fcs], SG2[0:32, :], start=False, stop=True), [g2b, SG2], [b6])
                    kb.op('dve', lambda e: e.tensor_copy(G[:, fc, :], b6[:, :]), [b6], [G])
                    a_ = AAt[:]
                    kb.op('dve', lambda e: e.tensor_scalar(tm1[:], t_k[:], self.P('k_k', hp), None, ALU.mult), [t_k, self.pvt], [tm1])
                    kb.op('act', lambda e: e.activation(tm4[:], tm1[:], AF.Square), [tm1], [tm4])
                    kb.op('pe', lambda e: e.matmul(b4[:, :], bones, tm4[:], start=True, stop=True), [self.cst, tm4], [b4])
                    self.rsqrt(tm4, b4[:, :], [b4], 1.0, 1e-24)
                    kb.op('dve', lambda e: e.tensor_tensor(tm1[:], tm1[:], tm4[:], ALU.mult), [tm1, tm4], [tm1])
                    kb.op('dve', lambda e: e.tensor_scalar(tm2[:], a_, self.P('k_a', hp), omka[:, hp:hp + 1], ALU.mult, ALU.add),
                          [AAt, self.pvt, omka], [tm2])
                    kb.op('dve', lambda e: e.tensor_tensor(tm2[:], tm2[:], t_k[:], ALU.mult), [tm2, t_k], [tm2])
                    kb.op('dve', lambda e: e.tensor_tensor(tm3[:], tm1[:], a_, ALU.mult), [tm1, AAt], [tm3])
                    kb.op('dve', lambda e: e.scalar_tensor_tensor(tm4[:], t_r[:], self.P('r_k', hp), tm2[:], ALU.mult, ALU.mult),
                          [t_r, self.pvt, tm2], [tm4])
                    kb.op('pe', lambda e: e.matmul(b5[:, :], bones, tm4[:], start=True, stop=True), [self.cst, tm4], [b5])
                    kb.op('dve', lambda e: e.tensor_tensor(BN[hp][:], b5[:, :], VV[hp][:], ALU.mult), [b5, VV[hp]], [BN[hp]])
                    kb.op('dve', lambda e: e.tensor_tensor_scan(tc[:], self.C('r64'), LWt[:], 0.0, ALU.mult, ALU.add),
                          [self.cst, LWt], [tc])
                    kb.op('act', lambda e: e.activation(tm4[:], tc[:], AF.Exp, scale=NEGE), [tc], [tm4])
                    kb.op('dve', lambda e: e.tensor_tensor(RH[hp][:], t_r[:], tm4[:], ALU.mult), [t_r, tm4], [RH[hp]])
                    kb.op('act', lambda e: e.activation(WE[:, hp, :], tc[:, CB - 1::CB], AF.Exp, scale=NEGE), [tc], [WE])
                    kb.op('act', lambda e: e.activation(tm4[:], tc[:], AF.Exp, scale=-NEGE), [tc], [tm4])
                    kb.op('dve', lambda e: e.tensor_tensor(KA[hp][:], tm2[:], tm4[:], ALU.mult), [tm2, tm4], [KA[hp]])
                    kb.op('dve', lambda e: e.scalar_tensor_tensor(BE[hp][:], tm3[:], -1.0, tm4[:], ALU.mult, ALU.mult),
                          [tm3, tm4], [BE[hp]])
                    kb.op('dve', lambda e: e.tensor_tensor(tc[:], tc[:], LWt[:], ALU.subtract), [tc, LWt], [tc])
                    kb.op('act', lambda e: e.activation(tm4[:], tc[:], AF.Exp, scale=NEGE), [tc], [tm4])
                    kb.op('dve', lambda e: e.tensor_tensor(AL[hp][:], tm1[:], tm4[:], ALU.mult), [tm1, tm4], [AL[hp]])
                def drain(gens):
                    gens = list(gens)
                    while gens:
                        for g_ in list(gens):
                            try:
                                next(g_)
                            except StopIteration:
                                gens.remove(g_)

                tts = {}
                drain([stage_a(0, (0, 1, 2, 3), 0, tts)])
                for ci in range(NCK):
                    nxt = {}
                    gens = [stage_b(ci, ci % 2, tts, par)]
                    if ci + 1 < NCK:
                        gens.insert(0, stage_a(ci + 1, (0, 1, 2, 3), (ci + 1) % 2, nxt))
                    drain(gens)
                    tts = nxt
                    par ^= 1
                for hp in range(4):
                    o = OSB[:, hp, :]
                    kb.op('pe', lambda e: e.matmul(b4[:, :], bones, o, start=True, stop=True), [self.cst, OSB], [b4])
                    kb.op('dve', lambda e: e.scalar_tensor_tensor(cen[:], b4[:, :], -1.0 / 64, o, ALU.mult, ALU.add), [b4, OSB], [cen])
                    kb.op('act', lambda e: e.activation(sqv[:], cen[:], AF.Square), [cen], [sqv])
                    kb.op('pe', lambda e: e.matmul(b5[:, :], bones, sqv[:], start=True, stop=True), [self.cst, sqv], [b5])
                    self.rsqrt(rstd, b5[:, :], [b5], 1.0 / 64, RWKV_LN_EPS)
                    kb.op('dve', lambda e: e.tensor_tensor(cen[:], cen[:], rstd[:], ALU.mult), [cen, rstd], [cen])
                    kb.op('dve', lambda e: e.tensor_scalar(cen[:], cen[:], self.P('ln_w', hp), self.P('ln_b', hp), ALU.mult, ALU.add),
                          [cen, self.pvt], [cen])
                    kb.op('pool', lambda e: e.tensor_tensor(cen[:], cen[:], BN[hp][:], ALU.add), [cen, BN[hp]], [cen])
                    y = rot(yb, hp)
                    kb.op('dve', lambda e: e.tensor_tensor(y[:], cen[:], G[:, hp, :], ALU.mult), [cen, G], [y])
                    kb.dma('pool', self.oT[1].h[hp, :, tok], y[:], [y], [self.oT[1]], y)


Prog.rwkv = _rwkv
```
